# Optimizing a Trainium2 kernel written in Bass

```python
import jax, jax.numpy as jnp
from jax import lax
import numpy as np

D_MODEL = 1024
BATCH = 8
SEQ = 2048
DEPTH = 4

GRID_W = 64
CTX_LEN = 256

GLA_HEADS = 4
GLA_DK = 64
GLA_DV = 128
GLA_KW = GLA_HEADS * GLA_DK
GLA_VW = GLA_HEADS * GLA_DV
GATE_RANK = 16
GATE_TAU = 16.0
GLA_GATE_BIAS = 2.0
GLA_CHUNK = 64
RMS_EPS = 1e-6
CONV_W = 512
CONV_K = 3
ATT_HEADS = 8
ATT_KV_HEADS = 2
HEAD_DIM = 64
ATT_GROUP = ATT_HEADS // ATT_KV_HEADS
ATT_QW = ATT_HEADS * HEAD_DIM
ATT_KVW = ATT_KV_HEADS * HEAD_DIM
WINDOW = 128
ATT_BLOCK = 128
ROPE_THETA = 10000.0
NEG_INF = -1e30
D_FF = 2816
FFN_CONV_K = 3
MEM_WIDTHS = (GLA_KW, GLA_VW, GATE_RANK, GATE_RANK, ATT_KVW, ATT_KVW)
REST_WIDTHS = (GLA_KW, GLA_VW, CONV_W, CONV_W, CONV_W, ATT_QW, D_MODEL, D_MODEL, D_MODEL)
MEM_COLS = sum(MEM_WIDTHS)
IN_COLS = MEM_COLS + sum(REST_WIDTHS)
LN_EPS = 1e-5
DEEPNORM_ALPHA = (2 * DEPTH) ** 0.25
DEEPNORM_BETA = (8 * DEPTH) ** -0.25

kernel_name = 'hybrid_gla_conv_swa_deepnorm_dit'


def _split(p, widths):
    return jnp.split(p, np.cumsum(widths)[:-1].tolist(), axis=-1)


def heads(a, n):
    return a.reshape(a.shape[:-1] + (n, a.shape[-1] // n))


def layer_norm(x, w, b):
    xf = x.astype(jnp.float32)
    mu = jnp.mean(xf, axis=-1, keepdims=True)
    var = jnp.mean(jnp.square(xf - mu), axis=-1, keepdims=True)
    y = (xf - mu) * lax.rsqrt(var + LN_EPS)
    return (y * w.astype(jnp.float32) + b.astype(jnp.float32)).astype(x.dtype)


def dwconv3(a, w):
    k, ch = w.shape
    return lax.conv_general_dilated(a, w[:, None, :].astype(a.dtype), window_strides=(1,),
                                    padding=((k // 2, k // 2),),
                                    dimension_numbers=('NWC', 'WIO', 'NWC'),
                                    feature_group_count=ch)


def axial_rope_angles(rows):
    row_idx = jnp.repeat(jnp.arange(rows), GRID_W).astype(jnp.float32)
    col_idx = jnp.tile(jnp.arange(GRID_W), rows).astype(jnp.float32)
    half = HEAD_DIM // 2
    inv_freq = ROPE_THETA ** (-jnp.arange(0, half, 2, dtype=jnp.float32) / half)
    return jnp.concatenate([row_idx[:, None] * inv_freq, col_idx[:, None] * inv_freq], axis=-1)


def apply_axial_rope(a, ang):
    af = a.astype(jnp.float32)
    cos = jnp.cos(ang)[None, :, None, :]
    sin = jnp.sin(ang)[None, :, None, :]
    q = HEAD_DIM // 4
    segs = []
    for axis in range(2):
        seg = af[..., axis * 2 * q:(axis + 1) * 2 * q]
        x1, x2 = seg[..., :q], seg[..., q:]
        c_, s_ = cos[..., axis * q:(axis + 1) * q], sin[..., axis * q:(axis + 1) * q]
        segs += [x1 * c_ - x2 * s_, x1 * s_ + x2 * c_]
    return jnp.concatenate(segs, axis=-1).astype(a.dtype)


def gla_heads(a):
    return heads(a, GLA_HEADS).astype(jnp.float32)


def gla_query(a):
    return gla_heads(a) * (GLA_DK ** -0.5)


def gla_log_decay(g_lr, w_up, bias):
    z = (g_lr @ w_up + bias).astype(jnp.float32)
    return heads(jax.nn.log_sigmoid(z) / GATE_TAU, GLA_HEADS)


def _chunk(a):
    bsz, t, h, e = a.shape
    return a.reshape(bsz, t // GLA_CHUNK, GLA_CHUNK, h, e)


def gla_chunk_states(kc, vc, gc, s0):
    b = jnp.cumsum(gc, axis=2)
    b_last = b[:, :, -1]
    upd = jnp.einsum('bnlhk,bnlhv->bnhkv', kc * jnp.exp(b_last[:, :, None] - b), vc)

    def step(s, inp):
        decay, u = inp
        return decay[..., None] * s + u, s

    s_final, s_starts = lax.scan(step, s0, (jnp.moveaxis(jnp.exp(b_last), 1, 0),
                                           jnp.moveaxis(upd, 1, 0)))
    return b, jnp.moveaxis(s_starts, 0, 1), s_final


def gla_scan(q, k, v, g, s0):
    bsz, t = q.shape[0], q.shape[1]
    qc, kc, vc, gc = _chunk(q), _chunk(k), _chunk(v), _chunk(g)
    b, s_starts, s_final = gla_chunk_states(kc, vc, gc, s0)
    q_dec = qc * jnp.exp(b)
    inter = jnp.einsum('bnlhk,bnhkv->bnlhv', q_dec, s_starts)
    scores = jnp.einsum('bnlhk,bnmhk->bnhlm', q_dec, kc * jnp.exp(-b))
    scores = jnp.where(jnp.tril(jnp.ones((GLA_CHUNK, GLA_CHUNK), dtype=bool)), scores, 0.0)
    intra = jnp.einsum('bnhlm,bnmhv->bnlhv', scores, vc)
    return (inter + intra).reshape(bsz, t, GLA_HEADS, GLA_DV), s_final


def gla_final_state(k, v, g, s0):
    _, _, s_final = gla_chunk_states(_chunk(k), _chunk(v), _chunk(g), s0)
    return s_final


def gla_bidir(q, k, v, g_fwd, g_bwd, s0_fwd, s0_bwd):
    o_f, s_f = gla_scan(q, k, v, g_fwd, s0_fwd)
    fl = lambda a: jnp.flip(a, axis=1)
    o_b, s_b = gla_scan(fl(q), fl(k), fl(v), fl(g_bwd), s0_bwd)
    return o_f + fl(o_b), s_f, s_b


def gla_output(o, r, norm_w):
    o = o * lax.rsqrt(jnp.mean(jnp.square(o), axis=-1, keepdims=True) + RMS_EPS) * norm_w.astype(jnp.float32)
    o = o.reshape(o.shape[:2] + (GLA_VW,)).astype(r.dtype)
    return o * jax.nn.silu(r)


def short_conv(h, gate_b, gate_c, w):
    return gate_b * dwconv3(gate_c * h, w)


def window_attention(q, k, v, k_ctx, v_ctx, sink):
    bsz, t = q.shape[0], q.shape[1]
    nb = t // ATT_BLOCK
    qb = q.reshape(bsz, nb, ATT_BLOCK, ATT_KV_HEADS, ATT_GROUP, HEAD_DIM)

    def band(a):
        ap = jnp.pad(a, ((0, 0), (ATT_BLOCK, ATT_BLOCK), (0, 0), (0, 0)))
        ap = ap.reshape(bsz, nb + 2, ATT_BLOCK, ATT_KV_HEADS, HEAD_DIM)
        return jnp.concatenate([ap[:, :-2], ap[:, 1:-1], ap[:, 2:]], axis=2)

    kb, vb = band(k), band(v)
    scale = HEAD_DIM ** -0.5
    s_loc = jnp.einsum('bnqhgd,bnkhd->bnhgqk', qb, kb).astype(jnp.float32) * scale
    blk = jnp.arange(nb)[:, None] * ATT_BLOCK
    qpos = blk + jnp.arange(ATT_BLOCK)[None, :]
    kpos = blk - ATT_BLOCK + jnp.arange(3 * ATT_BLOCK)[None, :]
    valid = ((kpos[:, None, :] >= 0) & (kpos[:, None, :] < t)
             & (jnp.abs(qpos[:, :, None] - kpos[:, None, :]) <= WINDOW))
    s_loc = jnp.where(valid[None, :, None, None], s_loc, NEG_INF)
    s_ctx = jnp.einsum('bnqhgd,bchd->bnhgqc', qb, k_ctx).astype(jnp.float32) * scale
    s_sink = jnp.broadcast_to(sink.astype(jnp.float32).reshape(1, 1, ATT_KV_HEADS, ATT_GROUP, 1, 1),
                              s_loc.shape[:-1] + (1,))
    p = jax.nn.softmax(jnp.concatenate([s_loc, s_ctx, s_sink], axis=-1), axis=-1).astype(v.dtype)
    n_loc = 3 * ATT_BLOCK
    n_ctx = k_ctx.shape[1]
    o = (jnp.einsum('bnhgqk,bnkhd->bnqhgd', p[..., :n_loc], vb)
         + jnp.einsum('bnhgqc,bchd->bnqhgd', p[..., n_loc:n_loc + n_ctx], v_ctx))
    return o.reshape(bsz, t, ATT_QW)


def context_attention(q, k, v, sink):
    scale = HEAD_DIM ** -0.5
    s = jnp.einsum('bqhgd,bkhd->bhgqk', q, k).astype(jnp.float32) * scale
    s_sink = jnp.broadcast_to(sink.astype(jnp.float32).reshape(1, ATT_KV_HEADS, ATT_GROUP, 1, 1),
                              s.shape[:-1] + (1,))
    p = jax.nn.softmax(jnp.concatenate([s, s_sink], axis=-1), axis=-1)[..., :-1].astype(v.dtype)
    o = jnp.einsum('bhgqk,bkhd->bqhgd', p, v)
    return o.reshape(o.shape[:2] + (ATT_QW,))


def merge_branches(ya, yb, yc, ma, mb, mc, w_a, w_b, w_c, w_o):
    m = jax.nn.sigmoid(ma) * (ya @ w_a) + jax.nn.sigmoid(mb) * (yb @ w_b) + jax.nn.sigmoid(mc) * (yc @ w_c)
    return m @ w_o


def conv_ffn(h, w_up, w_conv, w_down):
    u = dwconv3(h @ w_up, w_conv)
    gate, val = jnp.split(u, 2, axis=-1)
    return (jax.nn.silu(gate) * val) @ w_down


def setup_inputs(seed: int = 0) -> dict:
    key = jax.random.key(seed)
    ks = jax.random.split(key, 32)
    nrm = lambda k, shape, s: jax.random.normal(k, shape, jnp.float32) * s
    L = DEPTH
    return {
        'x': nrm(ks[0], (BATCH, SEQ, D_MODEL), 1.0),
        'c': nrm(ks[1], (BATCH, D_MODEL), 1.0),
        'ctx': nrm(ks[2], (BATCH, CTX_LEN, D_MODEL), 1.0),
        'c_ctx': nrm(ks[3], (D_MODEL,), 1.0),
        'w_ada': nrm(ks[4], (L, D_MODEL, 6 * D_MODEL), 0.5 * D_MODEL ** -0.5),
        'b_ada': nrm(ks[5], (L, 6 * D_MODEL), 0.02),
        'w_in': nrm(ks[6], (L, D_MODEL, IN_COLS), D_MODEL ** -0.5),
        'gla_gate_up_f': nrm(ks[7], (L, GATE_RANK, GLA_KW), GATE_RANK ** -0.5),
        'gla_gate_bias_f': GLA_GATE_BIAS + nrm(ks[8], (L, GLA_KW), 0.1),
        'gla_gate_up_b': nrm(ks[9], (L, GATE_RANK, GLA_KW), GATE_RANK ** -0.5),
        'gla_gate_bias_b': GLA_GATE_BIAS + nrm(ks[10], (L, GLA_KW), 0.1),
        'gla_norm_w': 1.0 + nrm(ks[11], (L, GLA_DV), 0.02),
        'conv_w': nrm(ks[12], (L, CONV_K, CONV_W), CONV_K ** -0.5),
        'att_sink': nrm(ks[13], (L, ATT_HEADS), 0.5),
        'w_branch_a': nrm(ks[14], (L, GLA_VW, D_MODEL), GLA_VW ** -0.5),
        'w_branch_b': nrm(ks[15], (L, CONV_W, D_MODEL), CONV_W ** -0.5),
        'w_branch_c': nrm(ks[16], (L, ATT_QW, D_MODEL), ATT_QW ** -0.5),
        'w_out': nrm(ks[17], (L, D_MODEL, D_MODEL), DEEPNORM_BETA * D_MODEL ** -0.5),
        'ln1_w': 1.0 + nrm(ks[18], (L, D_MODEL), 0.02),
        'ln1_b': nrm(ks[19], (L, D_MODEL), 0.02),
        'ffn_up': nrm(ks[20], (L, D_MODEL, 2 * D_FF), D_MODEL ** -0.5),
        'ffn_conv': nrm(ks[21], (L, FFN_CONV_K, 2 * D_FF), FFN_CONV_K ** -0.5),
        'ffn_down': nrm(ks[22], (L, D_FF, D_MODEL), DEEPNORM_BETA * D_FF ** -0.5),
        'ln2_w': 1.0 + nrm(ks[23], (L, D_MODEL), 0.02),
        'ln2_b': nrm(ks[24], (L, D_MODEL), 0.02),
    }


def reference(x, c, ctx, c_ctx, w_ada, b_ada, w_in, gla_gate_up_f, gla_gate_bias_f,
              gla_gate_up_b, gla_gate_bias_b, gla_norm_w, conv_w, att_sink,
              w_branch_a, w_branch_b, w_branch_c, w_out, ln1_w, ln1_b,
              ffn_up, ffn_conv, ffn_down, ln2_w, ln2_b):
    bsz, seq_len = x.shape[0], x.shape[1]
    rows = seq_len // GRID_W
    ang = axial_rope_angles(rows)
    s_zero = jnp.zeros((bsz, GLA_HEADS, GLA_DK, GLA_DV), jnp.float32)
    fl = lambda a: jnp.flip(a, axis=1)
    xl, xc = x, ctx
    for layer in range(DEPTH):
        last = layer == DEPTH - 1
        mod = jax.nn.silu(c) @ w_ada[layer] + b_ada[layer]
        sh1, sc1, g1, sh2, sc2, g2 = jnp.split(mod[:, None, :], 6, axis=-1)
        n_mod_c = (2 if last else 6) * D_MODEL
        mod_c = jax.nn.silu(c_ctx) @ w_ada[layer][:, :n_mod_c] + b_ada[layer][:n_mod_c]
        mod_c = jnp.split(mod_c, n_mod_c // D_MODEL)
        hl = xl * (1 + sc1) + sh1
        hc = xc * (1 + mod_c[1]) + mod_c[0]

        pc = hc @ (w_in[layer][:, :MEM_COLS] if last else w_in[layer])
        parts_c = _split(pc, MEM_WIDTHS if last else MEM_WIDTHS + REST_WIDTHS)
        kg_c, vg_c, gfr_c, gbr_c, ka_c, va_c = parts_c[:6]
        k_c, v_c = gla_heads(kg_c), gla_heads(vg_c)
        gf_c = gla_log_decay(gfr_c, gla_gate_up_f[layer], gla_gate_bias_f[layer])
        gb_c = gla_log_decay(gbr_c, gla_gate_up_b[layer], gla_gate_bias_b[layer])
        k_att_c = heads(ka_c, ATT_KV_HEADS)
        v_att_c = heads(va_c, ATT_KV_HEADS)
        if last:
            sf_c = gla_final_state(k_c, v_c, gf_c, s_zero)
            sb_c = gla_final_state(fl(k_c), fl(v_c), fl(gb_c), s_zero)
        else:
            qg_c, rg_c, ch_c, cb_c, cc_c, qa_c, ma_c, mb_c, mc_c = parts_c[6:]
            o_c, sf_c, sb_c = gla_bidir(gla_query(qg_c), k_c, v_c, gf_c, gb_c, s_zero, s_zero)
            ya_c = gla_output(o_c, rg_c, gla_norm_w[layer])
            yb_c = short_conv(ch_c, cb_c, cc_c, conv_w[layer])
            q_att_c = heads(qa_c, ATT_HEADS).reshape(qa_c.shape[:2] + (ATT_KV_HEADS, ATT_GROUP, HEAD_DIM))
            yc_c = context_attention(q_att_c, k_att_c, v_att_c, att_sink[layer])
            mix_c = merge_branches(ya_c, yb_c, yc_c, ma_c, mb_c, mc_c, w_branch_a[layer],
                                   w_branch_b[layer], w_branch_c[layer], w_out[layer])
            xc_mid = layer_norm(DEEPNORM_ALPHA * xc + mod_c[2] * mix_c, ln1_w[layer], ln1_b[layer])

        pl = hl @ w_in[layer]
        (kg, vg, gfr, gbr, ka, va, qg, rg, ch, cb, cc, qa, ma, mb, mc) = _split(pl, MEM_WIDTHS + REST_WIDTHS)
        gf = gla_log_decay(gfr, gla_gate_up_f[layer], gla_gate_bias_f[layer])
        gb = gla_log_decay(gbr, gla_gate_up_b[layer], gla_gate_bias_b[layer])
        o_l, _, _ = gla_bidir(gla_query(qg), gla_heads(kg), gla_heads(vg), gf, gb, sf_c, sb_c)
        ya = gla_output(o_l, rg, gla_norm_w[layer])
        yb = short_conv(ch, cb, cc, conv_w[layer])
        q_att = apply_axial_rope(heads(qa, ATT_HEADS), ang).reshape(
            bsz, seq_len, ATT_KV_HEADS, ATT_GROUP, HEAD_DIM)
        k_att = apply_axial_rope(heads(ka, ATT_KV_HEADS), ang)
        yc = window_attention(q_att, k_att, heads(va, ATT_KV_HEADS), k_att_c, v_att_c, att_sink[layer])
        mix = merge_branches(ya, yb, yc, ma, mb, mc, w_branch_a[layer], w_branch_b[layer],
                             w_branch_c[layer], w_out[layer])
        xl = layer_norm(DEEPNORM_ALPHA * xl + g1 * mix, ln1_w[layer], ln1_b[layer])

        f_l = conv_ffn(xl * (1 + sc2) + sh2, ffn_up[layer], ffn_conv[layer], ffn_down[layer])
        xl = layer_norm(DEEPNORM_ALPHA * xl + g2 * f_l, ln2_w[layer], ln2_b[layer])
        if not last:
            f_c = conv_ffn(xc_mid * (1 + mod_c[4]) + mod_c[3], ffn_up[layer], ffn_conv[layer], ffn_down[layer])
            xc = layer_norm(DEEPNORM_ALPHA * xc_mid + mod_c[5] * f_c, ln2_w[layer], ln2_b[layer])
    return xl
```

```python
import numpy as np
from contextlib import ExitStack
import concourse.bass as bass
import concourse.mybir as mybir
from concourse.bass_utils import run_bass_kernel_spmd

F32 = mybir.dt.float32
BF16 = mybir.dt.bfloat16
U8 = mybir.dt.uint8
AF = mybir.ActivationFunctionType
ALU = mybir.AluOpType

ENGS = ("pe", "act", "dve", "pool", "sp")

D = 1024
T = 2048
C = 256
NT = T + C
DEPTH = 4
KC = 8
DFF = 2816
NFF = DFF // 128
IN_COLS = 6944
ALPHA = (2 * DEPTH) ** 0.25
LN_EPS = 1e-5
RMS_EPS = 1e-6
TTS = [(0, 256), (256, 512), (768, 512), (1280, 512), (1792, 512)]
O_KG, O_VG, O_GF, O_GB, O_KA, O_VA = 0, 256, 768, 784, 800, 928
O_QG, O_RG, O_CH, O_CB, O_CC, O_QA, O_MA = 1056, 1312, 1824, 2336, 2848, 3360, 3872


class Prog:
    def __init__(self, nc, dry=False):
        self.nc = nc
        self.dry = dry
        self.ops = {e: [] for e in ENGS}
        self.last_w = {}
        self.readers = {}
        self.dma_sems = {}
        self.all_dma_tokens = {}

    def _deps_for(self, reads, writes):
        deps = []
        for k in reads:
            t = self.last_w.get(k)
            if t is not None:
                deps.append(t)
        for k in writes:
            t = self.last_w.get(k)
            if t is not None:
                deps.append(t)
            deps.extend(self.readers.get(k, ()))
        out, seen = [], set()
        for t in deps:
            if t not in seen:
                seen.add(t)
                out.append(t)
        return out

    def _update(self, tok, reads, writes):
        for k in reads:
            self.readers.setdefault(k, []).append(tok)
        for k in writes:
            self.last_w[k] = tok
            self.readers[k] = []

    def op(self, eng, fn, reads=(), writes=()):
        if self.dry:
            return
        idx = len(self.ops[eng])
        tok = ('E', eng, idx)
        fdeps = []
        for t in self._deps_for(reads, writes):
            if t[0] == 'E' and t[1] == eng:
                if eng == 'pe':
                    continue
            fdeps.append(t)
        self.ops[eng].append(dict(fn=fn, deps=fdeps, inc=False, kind='op'))
        self._update(tok, reads, writes)

    def dma(self, eng, fn, reads=(), writes=(), sem_key=None):
        if self.dry:
            return
        ent = self.dma_sems.setdefault(sem_key, [None, 0])
        ent[1] += 16
        tok = ('D', sem_key, ent[1])
        deps = [t for t in self._deps_for(reads, writes) if not (t[0] == 'D' and t[1] == sem_key)]
        self.ops[eng].append(dict(fn=fn, deps=deps, inc=False, kind='dma', sem_key=sem_key))
        self._update(tok, reads, writes)
        self.all_dma_tokens[sem_key] = tok

    def barrier(self, engs=("pe", "act", "dve", "sp")):
        if self.dry:
            return
        toks = [('E', e, len(self.ops[e]) - 1) for e in engs if self.ops[e]]
        dtoks = list(self.all_dma_tokens.values())
        for e in engs:
            deps = list(toks) + dtoks
            self.ops[e].append(dict(fn=None, deps=deps, inc=False, kind='wait'))

    def final_wait(self, eng="sp"):
        if self.dry:
            return
        deps = list(self.all_dma_tokens.values())
        for e in ENGS:
            if e != eng and self.ops[e]:
                deps.append(('E', e, len(self.ops[e]) - 1))
        self.ops[eng].append(dict(fn=None, deps=deps, inc=False, kind='wait'))

    def emit(self, stack):
        nc = self.nc

        def resolve(t):
            _, e, i = t
            while i >= 0 and self.ops[e][i]['kind'] != 'op':
                i -= 1
            return (e, i)

        for e in ENGS:
            for o in self.ops[e]:
                nd = []
                for t in o['deps']:
                    if t[0] == 'E':
                        e2, i2 = resolve(t)
                        if i2 < 0:
                            continue
                        self.ops[e2][i2]['inc'] = True
                        nd.append(('E', e2, i2))
                    else:
                        nd.append(t)
                o['deps'] = nd
        rank = {}
        for e in ENGS:
            r = 0
            for i, o in enumerate(self.ops[e]):
                if o['inc']:
                    r += 1
                    rank[(e, i)] = r
            assert r < 60000, (e, r)
        esem = {e: stack.enter_context(nc.semaphore("s_" + e)) for e in ENGS}
        for n, k in enumerate(self.dma_sems):
            self.dma_sems[k][0] = stack.enter_context(nc.semaphore("d%d" % n))
        block = stack.enter_context(nc.Block())
        prog = self

        def run(e, engine):
            waited = {}
            for o in prog.ops[e]:
                for t in o['deps']:
                    if t[0] == 'E':
                        sem, val, key = esem[t[1]], rank[(t[1], t[2])], ('E', t[1])
                    else:
                        sem, val, key = prog.dma_sems[t[1]][0], t[2], ('D', t[1])
                    if waited.get(key, 0) >= val:
                        continue
                    waited[key] = val
                    engine.wait_ge(sem, val)
                if o['fn'] is None:
                    continue
                ins = o['fn'](engine)
                if o['kind'] == 'dma':
                    ins.then_inc(prog.dma_sems[o['sem_key']][0], 16)
                elif o['inc']:
                    ins.then_inc(esem[e], 1)

        @block.tensor
        def _(eng):
            run("pe", eng)

        @block.scalar
        def _(eng):
            run("act", eng)

        @block.vector
        def _(eng):
            run("dve", eng)

        @block.gpsimd
        def _(eng):
            run("pool", eng)

        @block.sync
        def _(eng):
            run("sp", eng)


def I_mms(lst):
    def f(e):
        r = None
        for (o, l, rr, s, t) in lst:
            r = e.matmul(o, l, rr, start=s, stop=t)
        return r
    return f


def I_act(out, in_, func, bias=None, scale=None):
    kw = {}
    if bias is not None:
        kw['bias'] = bias
    if scale is not None:
        kw['scale'] = scale
    return lambda e: e.activation(out, in_, func, **kw)


def I_tt(out, a, b, op):
    return lambda e: e.tensor_tensor(out, a, b, op)


def I_ts(out, a, s1, s2, op0, op1=None):
    if op1 is None:
        return lambda e: e.tensor_scalar(out, a, s1, None, op0)
    return lambda e: e.tensor_scalar(out, a, s1, s2, op0, op1)


def I_stt(out, in0, scalar, in1, op0, op1):
    return lambda e: e.scalar_tensor_tensor(out, in0, scalar, in1, op0, op1)


def I_copy(out, in_):
    return lambda e: e.tensor_copy(out, in_)


def I_memset(ap, v):
    return lambda e: e.memset(ap, v)


def I_recip(out, in_):
    return lambda e: e.reciprocal(out, in_)


def I_dma(out, in_):
    return lambda e: e.dma_start(out=out, in_=in_)


def make_consts():
    m = np.arange(128)[:, None]
    l = np.arange(128)[None, :]
    same = (m // 64) == (l // 64)
    tri = -np.stack([same & (m <= l), same & (m >= l), same & (m > l), same & (m < l)], 1).astype(np.float32) / 16.0
    msk = np.stack([same & (l >= m), same & (l <= m), l <= m, m <= l], 1).astype(np.float32)
    perm = np.zeros((128, 5, 128), np.float32)
    sp = np.zeros((128, 128), np.float32)
    for pp in range(128):
        d = pp % 64
        i = d % 32
        if i < 16:
            sp[pp + 16, pp] = -1.0
        else:
            sp[pp - 16, pp] = 1.0
    perm[:, 0, :] = sp
    for kv in range(2):
        for pp in range(128):
            src = kv * 64 + pp % 64
            perm[src, 1 + kv, pp] = 1.0
            perm[:, 3 + kv, pp] = sp[:, src]
    ident = np.eye(128, dtype=np.float32)
    ones = np.ones((128, 128), np.float32)
    oneeo = np.zeros((128, 2, 128), np.float32)
    oneeo[:, 0, 0:64] = 1.0
    oneeo[:, 1, 64:128] = 1.0
    t = np.arange(T)
    row = (t // 64).astype(np.float32)
    col = (t % 64).astype(np.float32)
    inv_freq = (np.float32(10000.0) ** (-np.arange(0, 32, 2, dtype=np.float32) / np.float32(32))).astype(np.float32)
    ang = np.concatenate([row[:, None] * inv_freq[None, :], col[:, None] * inv_freq[None, :]], axis=1).astype(np.float32)
    rope = np.zeros((128, 2, NT), np.float32)
    rope[:, 0, :C] = 1.0
    for pp in range(128):
        d = pp % 64
        a = (d // 32) * 16 + (d % 16)
        rope[pp, 0, C:] = np.cos(ang[:, a])
        rope[pp, 1, C:] = np.sin(ang[:, a])
    return dict(c_tri=tri, c_msk=msk, c_perm=perm, c_ident=ident, c_ones=ones, c_oneeo=oneeo, c_rope=rope)


WEIGHT_SHAPES = dict(
    w_ada=[DEPTH, D, 6 * D], b_ada=[DEPTH, 6 * D], w_in=[DEPTH, D, IN_COLS],
    gla_gate_up_f=[DEPTH, 16, 256], gla_gate_bias_f=[DEPTH, 256],
    gla_gate_up_b=[DEPTH, 16, 256], gla_gate_bias_b=[DEPTH, 256],
    gla_norm_w=[DEPTH, 128], conv_w=[DEPTH, 3, 512], att_sink=[DEPTH, 8],
    w_branch_a=[DEPTH, 512, D], w_branch_b=[DEPTH, 512, D], w_branch_c=[DEPTH, 512, D],
    w_out=[DEPTH, D, D], ln1_w=[DEPTH, D], ln1_b=[DEPTH, D],
    ffn_up=[DEPTH, D, 2 * DFF], ffn_conv=[DEPTH, 3, 2 * DFF], ffn_down=[DEPTH, DFF, D],
    ln2_w=[DEPTH, D], ln2_b=[DEPTH, D],
)
CONST_SHAPES = dict(c_tri=[128, 4, 128], c_msk=[128, 4, 128], c_perm=[128, 5, 128], c_ident=[128, 128],
                    c_ones=[128, 128], c_oneeo=[128, 2, 128], c_rope=[128, 2, NT])

SC_LN1W, SC_LN1B, SC_LN2W, SC_LN2B = 0, 8, 16, 24
SC_CONV = 32
SC_FCONV = 44
SC_NORMW = 176
SC_A1 = 184
SC_AW1, SC_AB1 = 200, 208
SC_HW2, SC_HB2 = 216, 232
SC_AW2, SC_AB2 = 248, 256
SC_HW1N, SC_HB1N = 264, 280
SC_S1 = 296
SC_TMP = 312
SC_N = 320


class StopBuild(Exception):
    pass


def build_program(n_layers=DEPTH, taps=None, stop=None):
    nc = bass.Bass("TRN2", target_bir_lowering=False)
    taps = taps or {}
    dr = {}
    dr['x'] = nc.dram_tensor("x", [T, D], F32, kind="ExternalInput").ap()
    dr['ctx'] = nc.dram_tensor("ctx", [C, D], F32, kind="ExternalInput").ap()
    dr['cvec'] = nc.dram_tensor("cvec", [2, D], F32, kind="ExternalInput").ap()
    for k, s in WEIGHT_SHAPES.items():
        dr[k] = nc.dram_tensor(k, s, F32, kind="ExternalInput").ap()
    for k, s in CONST_SHAPES.items():
        dr[k] = nc.dram_tensor(k, s, F32, kind="ExternalInput").ap()
    out = nc.dram_tensor("out", [T, D], F32, kind="ExternalOutput").ap()
    XA = nc.dram_tensor("xa_scr", [KC, 128, NT], F32, kind="Internal").ap()
    tap_dr = {k: nc.dram_tensor("tap_" + k, list(s), F32, kind="ExternalOutput").ap() for k, s in taps.items()}

    off = [16640]
    OFFS = {}
    uniq = [0]

    def alloc(name, shape, dtype, nbytes=None):
        esz = 4 if dtype == F32 else 2
        n = int(np.prod(shape[1:])) * esz
        if nbytes is None:
            nbytes = n
        nbytes = (nbytes + 63) // 64 * 64
        t = nc.alloc_sbuf_tensor_at(name, list(shape), dtype, offset=off[0])
        OFFS[name] = off[0]
        off[0] += nbytes
        return t[:]

    class Region:
        def __init__(self, name, size):
            self.name, self.base, self.size, self.cur, self.n = name, off[0], size, 0, 0
            off[0] += size

        def reset(self):
            self.cur = 0

        def alloc(self, shape, dtype, at=None):
            esz = 4 if dtype == F32 else 2
            nb = (int(np.prod(shape[1:])) * esz + 63) // 64 * 64
            if at is None:
                at = self.cur
                self.cur += nb
            assert at + nb <= self.size, (self.name, at, nb, self.size)
            self.n += 1
            return nc.alloc_sbuf_tensor_at("%s_%d" % (self.name, self.n), list(shape), dtype, offset=self.base + at)[:]

    IDENT = alloc("ident", [128, 128], F32)
    ONES = alloc("ones", [128, 128], F32)
    TRI = alloc("tri", [128, 4, 128], F32)
    MSK = alloc("msk", [128, 4, 128], BF16)
    PERM = alloc("perm", [128, 5, 128], F32)
    ONEEO = alloc("oneeo", [128, 2, 128], BF16)
    ONEROW = alloc("onerow", [1, 16], F32)
    ROPE = alloc("rope", [128, 2, NT], BF16)
    MOD = alloc("mod", [128, DEPTH, 2, 48], F32)
    SC = alloc("sc", [128, DEPTH, SC_N], F32)
    ROWS = [alloc("row%d" % i, [8, 128], F32) for i in range(8)]
    WUP = alloc("wup", [33, 512], F32)
    SINK = alloc("sink", [128, 16], F32)
    SINKE = alloc("sinke", [128, 2, 2], F32)
    SCB = alloc("scb", [128, 8, 2], BF16)
    CV = alloc("cv", [128, 16], F32)
    Ht = alloc("H", [128, KC, NT], BF16)
    AR = Region("AR", NFF * NT * 2)
    WR = [alloc("wr%d" % i, [128, 4096], BF16) for i in range(3)]
    TMP = Region("TMP", 229376 - off[0])
    print('TMP size', TMP.size)
    assert TMP.size >= 18432, TMP.size
    PS = [nc.alloc_psum_tensor("ps%d" % i, [128, 512], F32)[:] for i in range(8)]
    PSK = ["ps%d" % i for i in range(8)]

    def build(P):
        try:
            return build_inner(P)
        except StopBuild:
            P.barrier()
            P.final_wait()
            return P._ws

    def build_inner(P):
        class WS:
            def __init__(self):
                self.descs = []
                self.next = 0
                self.issued = 0

            def get(self, parts, keep=0):
                if P.dry:
                    self.descs.append(parts)
                    return WR[0], "wr0"
                i = self.next
                self.next += 1
                lim = min(len(ALLW), i + 3 - keep)
                while self.issued < lim:
                    j = self.issued
                    s = j % 3
                    for (dv, src) in ALLW[j]:
                        P.dma("pool", I_dma(dv(WR[s]), src), writes=["wr%d" % s], sem_key="wr%d" % s)
                    self.issued += 1
                return WR[i % 3], "wr%d" % (i % 3)

        ws = WS()
        P._ws = ws
        if not P.dry:
            ws.descs = None

        def wsl(l, name, c0, n):
            src = dr[name][l, :, c0:c0 + n].rearrange("(kc p) c -> p kc c", p=128)
            return (lambda s, n=n: s[:, 0:8 * n].rearrange("p (kc c) -> p kc c", kc=8)), src

        bank = [0]
        reserved = set()

        def nb():
            while True:
                b = bank[0]
                bank[0] = (b + 1) % 8
                if b not in reserved:
                    return b

        def phase(name):
            if stop == name:
                raise StopBuild()

        def tap(name, ap, keys):
            if name in tap_dr:
                P.dma("pool", I_dma(tap_dr[name], ap), reads=keys, sem_key="tap_" + name)

        for nm, dst in (("c_ident", IDENT), ("c_ones", ONES), ("c_tri", TRI), ("c_perm", PERM)):
            P.dma("sp", I_dma(dst, dr[nm]), writes=[nm], sem_key=nm)
        for nm, dst in (("c_msk", MSK), ("c_oneeo", ONEEO), ("c_rope", ROPE)):
            P.dma("pool", I_dma(dst, dr[nm]), writes=[nm], sem_key=nm)
        P.op("dve", I_memset(ONEROW, 1.0), writes=["onerow"])

        rowi = [0]

        def row_to_cols(src_row_ap, n, dst_ap, dkey):
            nchk = n // 128
            b = nb()
            q = rowi[0]
            rowi[0] = (q + 1) % 8
            rk = "row%d" % q
            P.dma("sp", I_dma(ROWS[q][0:nchk, :], src_row_ap.rearrange("o (a b) -> (o a) b", b=128)), writes=[rk], sem_key=rk)
            P.op("pe", I_mms([(PS[b][:, 0:nchk], ROWS[q][0:nchk, :], IDENT[0:nchk, 0:nchk], True, True)]),
                 reads=[rk, "c_ident"], writes=[PSK[b]])
            P.op("dve", I_copy(dst_ap, PS[b][:, 0:nchk]), reads=[], writes=[PSK[b], dkey])

        for w in range(2):
            row_to_cols(dr['cvec'][w:w + 1, :], 1024, CV[:, w * 8:(w + 1) * 8], "cv")
        for w in range(2):
            P.op("act", I_act(SCB[:, :, w], CV[:, w * 8:(w + 1) * 8], AF.Silu), reads=["cv"], writes=["scb"])
        ROW2 = [TMP.alloc([2, 512], F32, at=1024 + i * 2048) for i in range(2)]
        for l in range(n_layers):
            bA = nb()
            reserved.add(bA)
            for s in range(12):
                vf, src = wsl(l, 'w_ada', s * 512, 512)
                slot, sk = ws.get([(vf, src)])
                W = vf(slot)
                bt = nb()
                q2 = s % 2
                P.op("pe", I_mms([(PS[bt][0:2, 0:512], SCB[:, kc, :], W[:, kc, :], kc == 0, kc == 7) for kc in range(8)]),
                     reads=[sk, "scb"], writes=[PSK[bt]])
                P.op("dve", I_copy(ROW2[q2], PS[bt][0:2, 0:512]), writes=[PSK[bt], "rowb%d" % q2])
                P.op("pe", I_mms([(PS[bA][:, 2 * (s * 4 + j):2 * (s * 4 + j) + 2], ROW2[q2][0:2, j * 128:(j + 1) * 128], IDENT[0:2, 0:2], True, True)
                                  for j in range(4)]), reads=["rowb%d" % q2, "c_ident"], writes=[PSK[bA]])
            BT = TMP.alloc([128, 48], F32, at=0)
            for pc in range(6):
                row_to_cols(dr['b_ada'][l:l + 1, pc * 1024:(pc + 1) * 1024], 1024, BT[:, pc * 8:(pc + 1) * 8], "bt")
            pa = PS[bA][:, 0:96].rearrange("p (n w) -> p n w", w=2)
            for w in range(2):
                P.op("dve", I_tt(MOD[:, l, w, :], pa[:, :, w], BT[:, :], ALU.add), reads=["bt"], writes=[PSK[bA], "mod%d" % l])
            reserved.discard(bA)
        P.barrier()

        tap('mod', MOD[:, 0, :, :], ['mod0'])
        phase('ada')
        def sc(l, o, n=8):
            return SC[:, l, o:o + n]

        for l in range(n_layers):
            for nm, o in (("ln1_w", SC_LN1W), ("ln1_b", SC_LN1B), ("ln2_w", SC_LN2W), ("ln2_b", SC_LN2B)):
                row_to_cols(dr[nm][l:l + 1, :], 1024, sc(l, o), "sc%d" % l)
            cw = dr['conv_w'][l].rearrange("k c -> (k c)").unsqueeze(0)
            row_to_cols(cw[:, 0:1024], 1024, SC[:, l, SC_CONV:SC_CONV + 8], "sc%d" % l)
            row_to_cols(cw[:, 1024:1536], 512, SC[:, l, SC_CONV + 8:SC_CONV + 12], "sc%d" % l)
            fw = dr['ffn_conv'][l].rearrange("k c -> (k c)").unsqueeze(0)
            for pc in range(17):
                n = 1024 if pc < 16 else 512
                row_to_cols(fw[:, pc * 1024:pc * 1024 + n], n, SC[:, l, SC_FCONV + pc * 8:SC_FCONV + pc * 8 + n // 128], "sc%d" % l)
            row_to_cols(dr['gla_norm_w'][l:l + 1, :], 128, SC[:, l, SC_NORMW:SC_NORMW + 1], "sc%d" % l)
        for l in range(n_layers):
            k = ["sc%d" % l, "mod%d" % l]
            wk = ["sc%d" % l]
            for w in range(2):
                md = MOD[:, l, w, :]
                P.op("dve", I_ts(sc(l, SC_S1 + 8 * w), md[:, 8:16], 1.0, None, ALU.add), reads=k, writes=wk)
                P.op("dve", I_ts(sc(l, SC_A1 + 8 * w), md[:, 8:16], 1.0, 1.0 / ALPHA, ALU.add, ALU.mult), reads=k, writes=wk)
                P.op("dve", I_ts(sc(l, SC_TMP), md[:, 32:40], 1.0, None, ALU.add), reads=k, writes=wk)
                P.op("dve", I_tt(sc(l, SC_HW2 + 8 * w), sc(l, SC_LN1W), sc(l, SC_TMP), ALU.mult), reads=k, writes=wk)
                P.op("dve", I_tt(sc(l, SC_HB2 + 8 * w), sc(l, SC_LN1B), sc(l, SC_TMP), ALU.mult), reads=k, writes=wk)
                P.op("dve", I_tt(sc(l, SC_HB2 + 8 * w), sc(l, SC_HB2 + 8 * w), md[:, 24:32], ALU.add), reads=k, writes=wk)
            P.op("dve", I_ts(sc(l, SC_AW1), sc(l, SC_LN1W), ALPHA, None, ALU.mult), reads=k, writes=wk)
            P.op("dve", I_ts(sc(l, SC_AB1), sc(l, SC_LN1B), ALPHA, None, ALU.mult), reads=k, writes=wk)
            P.op("dve", I_ts(sc(l, SC_AW2), sc(l, SC_LN2W), ALPHA, None, ALU.mult), reads=k, writes=wk)
            P.op("dve", I_ts(sc(l, SC_AB2), sc(l, SC_LN2B), ALPHA, None, ALU.mult), reads=k, writes=wk)
        for l in range(n_layers - 1):
            k = ["sc%d" % l, "sc%d" % (l + 1)]
            wk = ["sc%d" % l]
            for w in range(2):
                P.op("dve", I_tt(sc(l, SC_HW1N + 8 * w), sc(l, SC_LN2W), sc(l + 1, SC_S1 + 8 * w), ALU.mult), reads=k, writes=wk)
                P.op("dve", I_tt(sc(l, SC_HB1N + 8 * w), sc(l, SC_LN2B), sc(l + 1, SC_S1 + 8 * w), ALU.mult), reads=k, writes=wk)
                P.op("dve", I_tt(sc(l, SC_HB1N + 8 * w), sc(l, SC_HB1N + 8 * w), MOD[:, l + 1, w, 0:8], ALU.add),
                     reads=k + ["mod%d" % (l + 1)], writes=wk)
        P.barrier()

        tap('sc', SC[:, 0, :], ['sc0'])
        phase('sc')
        TMP.reset()
        XT = [TMP.alloc([128, 1024], F32) for _ in range(2)]
        XS = [TMP.alloc([128, 8, 128], F32) for _ in range(2)]
        def in_load(j):
            src = dr['ctx'][j * 128:(j + 1) * 128, :] if j < 2 else dr['x'][(j - 2) * 128:(j - 1) * 128, :]
            q = j % 2
            P.dma("sp", I_dma(XT[q], src), writes=["xt%d" % q], sem_key="xt%d" % q)

        in_load(0)
        in_load(1)
        for j in range(18):
            w = 1 if j < 2 else 0
            q = j % 2
            bs = [nb(), nb()]
            for hb in range(2):
                P.op("pe", I_mms([(PS[bs[hb]][:, c * 128:(c + 1) * 128], XT[q][:, (hb * 4 + c) * 128:(hb * 4 + c + 1) * 128], IDENT, True, True)
                                  for c in range(4)]), reads=["xt%d" % q, "c_ident"], writes=[PSK[bs[hb]]])
            for kc in range(8):
                pv = PS[bs[kc // 4]][:, (kc % 4) * 128:(kc % 4 + 1) * 128]
                P.op("act", I_act(Ht[:, kc, j * 128:(j + 1) * 128], pv, AF.Identity,
                                  bias=MOD[:, 0, w, kc:kc + 1], scale=SC[:, 0, SC_S1 + 8 * w + kc:SC_S1 + 8 * w + kc + 1]),
                     reads=["sc0", "mod0"], writes=[PSK[bs[kc // 4]], "H%d" % j])
            for hb in range(2):
                P.op("dve", I_ts(XS[q][:, hb * 4:(hb + 1) * 4, :], PS[bs[hb]][:, :].rearrange("p (c t) -> p c t", c=4), ALPHA, None, ALU.mult),
                     writes=[PSK[bs[hb]], "xs%d" % q])
            P.dma("sp", I_dma(XA[:, :, j * 128:(j + 1) * 128].rearrange("kc p t -> p kc t"), XS[q]), reads=["xs%d" % q],
                  writes=["XA%d" % j], sem_key="xs%d" % q)
            if j + 2 < 18:
                in_load(j + 2)
        P.barrier()

        tap('h', Ht, ['H%d' % j for j in range(18)])
        phase('in')
        def hkeys(t0, n):
            return ["H%d" % j for j in range(t0 // 128, (t0 + n) // 128)]

        def xakeys(t0, n):
            return ["XA%d" % j for j in range(t0 // 128, (t0 + n) // 128)]

        def fm_group(bk, W_kc, src_kc, nk, t0, n, M=128):
            return I_mms([(PS[bk][0:M, 0:n], W_kc(kc), src_kc(kc)[:, t0:t0 + n], kc == 0, kc == nk - 1) for kc in range(nk)])

        def Hk(kc):
            return Ht[:, kc, :]

        for l in range(n_layers):
            last = (l == DEPTH - 1)
            tts = TTS[1:] if last else TTS
            tiles_out = list(range(2, 18)) if last else list(range(18))
            sck = "sc%d" % l
            AR.reset()
            TMP.reset()
            OB = AR.alloc([128, 4, NT], BF16, at=0)
            KR = AR.alloc([128, 2, NT], BF16, at=18432)
            VAE = AR.alloc([128, 18, 2, 128], BF16, at=27648)
            VAO = AR.alloc([128, 18, 2, 128], BF16, at=36864)
            G0 = 46080
            KGT = AR.alloc([128, 2, NT], BF16, at=G0)
            QGT = AR.alloc([128, 2, NT], BF16, at=G0 + 9216)
            GT = AR.alloc([33, NT], F32, at=G0 + 18432)
            KGTM = AR.alloc([128, 18, 256], BF16, at=G0 + 27648)
            VGTM = AR.alloc([128, 18, 512], BF16, at=G0 + 36864)
            KRAW = TMP.alloc([128, 512], F32)
            T1 = TMP.alloc([128, 512], F32)
            T2 = TMP.alloc([128, 512], F32)
            P.op("dve", I_memset(GT[32:33, :], 1.0), writes=["gt"])
            P.op("dve", I_memset(VAE, 0.0), writes=["vae"])
            P.op("dve", I_memset(VAO, 0.0), writes=["vao"])
            P.op("dve", I_memset(WUP, 0.0), writes=["wup"])
            P.dma("sp", I_dma(WUP[0:16, 0:256], dr['gla_gate_up_f'][l]), writes=["wup"], sem_key="wup")
            P.dma("sp", I_dma(WUP[16:32, 256:512], dr['gla_gate_up_b'][l]), writes=["wup"], sem_key="wup")
            P.dma("sp", I_dma(WUP[32:33, 0:256], dr['gla_gate_bias_f'][l:l + 1, :]), writes=["wup"], sem_key="wup")
            P.dma("sp", I_dma(WUP[32:33, 256:512], dr['gla_gate_bias_b'][l:l + 1, :]), writes=["wup"], sem_key="wup")
            P.dma("sp", I_dma(SINK[:, 0:8], dr['att_sink'][l:l + 1, :].partition_broadcast(128)), writes=["sink"], sem_key="sink")
            P.op("act", I_act(SINK[:, 8:16], SINK[:, 0:8], AF.Exp), reads=["sink"], writes=["sink"])
            for kv in range(2):
                P.op("dve", I_copy(SINKE[0:64, kv, :], SINK[0:64, 8 + kv * 4:8 + kv * 4 + 4:2]), reads=["sink"], writes=["sinke"])
                P.op("dve", I_copy(SINKE[64:128, kv, :], SINK[64:128, 8 + kv * 4 + 1:8 + kv * 4 + 4:2]), reads=["sink"], writes=["sinke"])

            vA, srcA = wsl(l, 'w_in', 0, 512)
            slotA, kA = ws.get([(vA, srcA)])
            WA = vA(slotA)
            for (t0, n) in TTS:
                for c in range(2):
                    b = nb()
                    P.op("pe", fm_group(b, lambda kc, c=c: WA[:, kc, c * 128:(c + 1) * 128], Hk, 8, t0, n),
                         reads=[kA] + hkeys(t0, n), writes=[PSK[b]])
                    P.op("act", I_act(KGT[:, c, t0:t0 + n], PS[b][:, 0:n], AF.Copy), writes=[PSK[b], "kgt"])
            for j in range(18):
                b = nb()
                P.op("pe", I_mms([(PS[b][:, 0:512], Ht[:, kc, j * 128:(j + 1) * 128], WA[:, kc, :], kc == 0, kc == 7) for kc in range(8)]),
                     reads=[kA, "H%d" % j], writes=[PSK[b]])
                P.op("act", I_act(KGTM[:, j, :], PS[b][:, 0:256], AF.Copy), writes=[PSK[b], "kgtm"])
                P.op("dve", I_copy(VGTM[:, j, 0:256], PS[b][:, 256:512]), writes=[PSK[b], "vgtm"])
            vB, srcB = wsl(l, 'w_in', 512, 512)
            slotB, kB = ws.get([(vB, srcB)])
            WB = vB(slotB)
            vC, srcC = wsl(l, 'w_in', 1024, 288)
            slotC, kCk = ws.get([(vC, srcC)], keep=1)
            WC = vC(slotC)
            for j in range(18):
                b = nb()
                lst = [(PS[b][:, 0:256], Ht[:, kc, j * 128:(j + 1) * 128], WB[:, kc, 0:256], kc == 0, kc == 7) for kc in range(8)]
                lst += [(PS[b][:, 256:352], Ht[:, kc, j * 128:(j + 1) * 128], WB[:, kc, 416:512], kc == 0, kc == 7) for kc in range(8)]
                lst += [(PS[b][:, 352:384], Ht[:, kc, j * 128:(j + 1) * 128], WC[:, kc, 0:32], kc == 0, kc == 7) for kc in range(8)]
                P.op("pe", I_mms(lst), reads=[kB, kCk, "H%d" % j], writes=[PSK[b]])
                P.op("dve", I_copy(VGTM[:, j, 256:512], PS[b][:, 0:256]), writes=[PSK[b], "vgtm"])
                pv = PS[b][:, 256:384].rearrange("p (k d) -> p k d", k=2)
                P.op("act", I_act(VAE[:, j, :, 0:64], pv, AF.Copy), writes=[PSK[b], "vae"])
                P.op("act", I_act(VAO[:, j, :, 64:128], pv, AF.Copy), writes=[PSK[b], "vao"])
            for (t0, n) in TTS:
                b = nb()
                P.op("pe", fm_group(b, lambda kc: WB[:, kc, 256:288], Hk, 8, t0, n, M=32), reads=[kB] + hkeys(t0, n), writes=[PSK[b]])
                P.op("act", I_act(GT[0:32, t0:t0 + n], PS[b][0:32, 0:n], AF.Copy), writes=[PSK[b], "gt"])
                for c in range(2):
                    b = nb()
                    P.op("pe", fm_group(b, lambda kc, c=c: WC[:, kc, 32 + c * 128:32 + (c + 1) * 128], Hk, 8, t0, n),
                         reads=[kCk] + hkeys(t0, n), writes=[PSK[b]])
                    P.op("act", I_act(QGT[:, c, t0:t0 + n], PS[b][:, 0:n], AF.Copy, scale=0.125), writes=[PSK[b], "qgt"])
                b = nb()
                P.op("pe", fm_group(b, lambda kc: WB[:, kc, 288:416], Hk, 8, t0, n), reads=[kB] + hkeys(t0, n), writes=[PSK[b]])
                P.op("act", I_act(KRAW[:, 0:n], PS[b][:, 0:n], AF.Copy), writes=[PSK[b], "kraw"])
                for kv in range(2):
                    bd, br_ = nb(), nb()
                    P.op("pe", I_mms([(PS[bd][:, 0:n], PERM[:, 1 + kv, :], KRAW[:, 0:n], True, True)]), reads=["kraw", "c_perm"], writes=[PSK[bd]])
                    P.op("pe", I_mms([(PS[br_][:, 0:n], PERM[:, 3 + kv, :], KRAW[:, 0:n], True, True)]), reads=["kraw", "c_perm"], writes=[PSK[br_]])
                    P.op("dve", I_tt(T1[:, 0:n], PS[bd][:, 0:n], ROPE[:, 0, t0:t0 + n], ALU.mult), reads=["c_rope"], writes=[PSK[bd], "t1"])
                    P.op("dve", I_tt(T2[:, 0:n], PS[br_][:, 0:n], ROPE[:, 1, t0:t0 + n], ALU.mult), reads=["c_rope"], writes=[PSK[br_], "t2"])
                    P.op("dve", I_tt(KR[:, kv, t0:t0 + n], T1[:, 0:n], T2[:, 0:n], ALU.add), reads=["t1", "t2"], writes=["kr"])
            P.barrier()

            if l == 0:
                tap('kgt', KGT, ['kgt']); tap('qgt', QGT, ['qgt']); tap('gt', GT, ['gt']); tap('kgtm', KGTM, ['kgtm'])
                tap('vgtm', VGTM, ['vgtm']); tap('kr', KR, ['kr']); tap('vae', VAE, ['vae']); tap('vao', VAO, ['vao'])
            phase('c1_%d' % l)
            TMP.reset()

            def gla_tmpset(alloc_fn):
                T_ = {}
                for nm, shp, dt_ in (("AZ", [128, 256], F32), ("GP", [128, 256], F32), ("EXE", [128, 256], F32), ("EB0", [128, 256], F32),
                                     ("ENB", [128, 256], F32), ("KDEC", [128, 2, 128], BF16), ("SBF", [128, 2, 2, 128], BF16),
                                     ("S", [128, 2, 128], F32), ("EB1", [128, 256], F32),
                                     ("KUPD4", [128, 2, 2, 2, 128], BF16),
                                     ("QBD0", [128, 2, 2, 128], BF16), ("QBD1", [128, 2, 2, 128], BF16),
                                     ("ST0", [128, 4, 128], BF16), ("ST1", [128, 4, 128], BF16),
                                     ("SQB", [128, 512], BF16), ("SQ", [128, 512], F32), ("OS", [128, 4, 128], F32)):
                    T_[nm] = alloc_fn(nm, shp, dt_)
                T_["RS"] = T_["SQ"]
                return T_

            Tf = gla_tmpset(lambda nm, shp, dt_: TMP.alloc(shp, dt_))
            if P.dry:
                sl1, sl2 = 1, 2
            else:
                sl1, sl2 = (ws.next - 2) % 3, (ws.next - 1) % 3
            cur = {"a": OFFS["wr%d" % sl1], "b": OFFS["wr%d" % sl2], "c": OFFS["row0"]}
            lim_ = {"a": OFFS["wr%d" % sl1] + 8192, "b": OFFS["wr%d" % sl2] + 8192, "c": OFFS["row0"] + 4096}
            place = dict(AZ="a", GP="a", EXE="a", EB0="a", ENB="a", KDEC="a", SBF="a", S="a", EB1="b", KUPD4="b",
                         QBD0="b", QBD1="b", ST0="b", ST1="b", SQB="b", SQ="c", OS="c")
            uniq[0] += 1

            def alloc_b(nm, shp, dt_):
                r_ = place[nm]
                nbytes = int(np.prod(shp[1:])) * (4 if dt_ == F32 else 2)
                t_ = nc.alloc_sbuf_tensor_at("glb_%s_%d" % (nm, uniq[0]), list(shp), dt_, offset=cur[r_])[:]
                cur[r_] += nbytes
                assert cur[r_] <= lim_[r_], (nm, r_)
                return t_

            Tb = gla_tmpset(alloc_b)
            alias_keys = ["wr%d" % sl1, "wr%d" % sl2]
            for d, T_ in ((0, Tf), (1, Tb)):
                P.op("dve", I_memset(T_["S"], 0.0), writes=["S%d_%d" % (d, cc_) for cc_ in range(2)])
                P.op("dve", I_memset(T_["KUPD4"], 0.0), writes=["kupd%d" % d])
                for p_ in range(2):
                    P.op("dve", I_memset(T_["QBD%d" % p_], 0.0), writes=["qdec%d%d" % (p_, d)])
            order_f = list(range(18))
            order_b = [1, 0] + list(range(17, 1, -1))
            step_of = {0: {j: i for i, j in enumerate(order_f)}, 1: {j: i for i, j in enumerate(order_b)}}

            def gla_tile(d, j, T_):
                par = step_of[d][j] % 2
                dbl = ("eb", "qdec", "st")
                k_ = lambda nm: ("%s%d%d" % (nm, par, d)) if nm in dbl else ("%s%d" % (nm, d))
                AZ, GP, EXE, ENB, KDEC = T_["AZ"], T_["GP"], T_["EXE"], T_["ENB"], T_["KDEC"]
                EB, KUPD4, QBD, ST = T_["EB%d" % par], T_["KUPD4"], T_["QBD%d" % par], T_["ST%d" % par]
                SBF, S, SQ, RS, OS, SQB = T_["SBF"], T_["S"], T_["SQ"], T_["RS"], T_["OS"], T_["SQB"]
                cs = slice(j * 128, (j + 1) * 128)
                do_out = j in tiles_out
                so, st_ = step_of[1 - d][j], step_of[d][j]
                second = (so < st_) or (so == st_ and d == 0)
                bz = nb()
                P.op("pe", I_mms([(PS[bz][:, 0:256], GT[0:33, cs], WUP[0:33, d * 256:(d + 1) * 256], True, True)]),
                     reads=["gt", "wup"], writes=[PSK[bz]])
                P.op("act", I_act(AZ, PS[bz][:, 0:256], AF.Exp, scale=-1.0), writes=[PSK[bz], k_("az")])
                P.op("act", I_act(GP, AZ, AF.Ln, bias=1.0), reads=[k_("az")], writes=[k_("gp")])
                yield 1
                be = nb()
                P.op("pe", I_mms([(PS[be][:, 0:256], TRI[:, 2 + d, :], GP, True, True)] +
                                 [(PS[be][:, 256 + c * 128:256 + (c + 1) * 128], GP[:, c * 128:(c + 1) * 128], TRI[:, d, :], True, True) for c in range(2)]),
                     reads=[k_("gp"), "c_tri"], writes=[PSK[be]])
                P.op("act", I_act(EXE, PS[be][:, 0:256], AF.Exp), writes=[PSK[be], k_("exe")])
                P.op("act", I_act(EB, PS[be][:, 256:512], AF.Exp), writes=[PSK[be], k_("eb")])
                if do_out:
                    P.op("act", I_act(ENB, PS[be][:, 256:512], AF.Exp, scale=-1.0), writes=[PSK[be], k_("enb")])
                yield 2
                for n_ in range(2):
                    for hf in range(2):
                        rows = slice(n_ * 64, (n_ + 1) * 64)
                        P.op("dve", I_tt(KUPD4[rows, n_, :, hf, hf * 64:(hf + 1) * 64],
                                         KGTM[rows, j, :].rearrange("p (cc hf k) -> p cc hf k", cc=2, hf=2)[:, :, hf, :],
                                         EXE[rows, :].rearrange("p (cc hf k) -> p cc hf k", cc=2, hf=2)[:, :, hf, :], ALU.mult),
                             reads=["kgtm", k_("exe")], writes=[k_("kupd")])
                if do_out:
                    for hf in range(2):
                        P.op("dve", I_tt(QBD[hf * 64:(hf + 1) * 64, :, hf, :], QGT[hf * 64:(hf + 1) * 64, :, cs],
                                         EB[hf * 64:(hf + 1) * 64, :].rearrange("p (c t) -> p c t", c=2), ALU.mult), reads=["qgt", k_("eb")], writes=[k_("qdec")])
                    P.op("dve", I_tt(KDEC, KGT[:, :, cs], ENB.rearrange("p (c t) -> p c t", c=2), ALU.mult), reads=["kgt", k_("enb")], writes=[k_("kdec")])
                    bs_ = nb()
                    P.op("pe", I_mms([(PS[bs_][:, cc * 256:(cc + 1) * 256], KDEC[:, cc, :], QBD[:, cc, :, :], True, True) for cc in range(2)]),
                         reads=[k_("kdec"), k_("qdec")], writes=[PSK[bs_]])
                    P.op("dve", I_tt(ST, PS[bs_][:, :].rearrange("p (h t) -> p h t", h=4),
                                     MSK[:, d, :].unsqueeze(1).broadcast_to([128, 4, 128]), ALU.mult), reads=["c_msk"], writes=[PSK[bs_], k_("st")])
                yield 3
                bu = [nb()]
                reserved.update(bu)
                lstu = []
                for cc in range(2):
                    for n_ in range(2):
                        blk = cc * 2 + n_
                        for hf in range(2):
                            lstu.append((PS[bu[0]][:, blk * 128:(blk + 1) * 128], KUPD4[:, n_, cc, hf, :],
                                         VGTM[:, j, (cc * 2 + hf) * 128:(cc * 2 + hf + 1) * 128], hf == 0, hf == 1))
                P.op("pe", I_mms(lstu), reads=[k_("kupd"), "vgtm"], writes=[PSK[bu[0]]])
                yield 4
                co = [0, 1] if d == 0 else [1, 0]
                for n_ in co:
                    yield 5
                    if do_out:
                        P.op("act", I_act(SBF[:, :, n_, :], S, AF.Copy), reads=["S%d_%d" % (d, cc_) for cc_ in range(2)],
                             writes=[k_("sbf%d" % n_)])
                    lcol = (63 if d == 0 else 0) + n_ * 64
                    for cc in range(2):
                        blk = cc * 2 + n_
                        P.op("dve", I_stt(S[:, cc, :], S[:, cc, :], EB[:, cc * 128 + lcol:cc * 128 + lcol + 1],
                                          PS[bu[0]][:, blk * 128:(blk + 1) * 128], ALU.mult, ALU.add),
                             reads=[k_("eb")], writes=[PSK[bu[0]], "S%d_%d" % (d, cc)])
                reserved.difference_update(bu)
                if not do_out:
                    return
                yield 6
                bo = nb()
                lst = []
                for h in range(4):
                    cc = h // 2
                    lst.append((PS[bo][:, h * 128:(h + 1) * 128], VGTM[:, j, h * 128:(h + 1) * 128], ST[:, h, :], True, False))
                    for n_ in range(2):
                        lst.append((PS[bo][:, h * 128 + n_ * 64:h * 128 + n_ * 64 + 64], SBF[:, cc, n_, :],
                                    QBD[:, cc, h % 2, n_ * 64:(n_ + 1) * 64], False, n_ == 1))
                P.op("pe", I_mms(lst), reads=["vgtm", k_("st"), k_("sbf0"), k_("sbf1"), k_("qdec")], writes=[PSK[bo]])
                pso = PS[bo][:, :].rearrange("p (h t) -> p h t", h=4)
                if not second:
                    P.op("act", I_act(OB[:, :, cs], pso, AF.Copy), writes=[PSK[bo], "ob%d" % j])
                else:
                    P.op("dve", I_tt(OS, pso, OB[:, :, cs], ALU.add), reads=["ob%d" % j], writes=[PSK[bo], k_("os")])
                    P.op("dve", I_tt(SQB, OS.rearrange("p h t -> p (h t)"), OS.rearrange("p h t -> p (h t)"), ALU.mult), reads=[k_("os")], writes=[k_("sq")])
                    bn_ = nb()
                    P.op("pe", I_mms([(PS[bn_][:, :], ONEEO[:, 0, :], SQB, True, False), (PS[bn_][:, :], ONEEO[:, 1, :], SQB, False, True)]),
                         reads=[k_("sq"), "c_oneeo"], writes=[PSK[bn_]])
                    P.op("act", I_act(RS, PS[bn_][:, :], AF.Ln, bias=RMS_EPS, scale=1.0 / 128.0), writes=[PSK[bn_], k_("rs")])
                    P.op("act", I_act(RS, RS, AF.Exp, scale=-0.5), writes=[k_("rs")])
                    P.op("dve", I_stt(OB[:, :, cs], OS, SC[:, l, SC_NORMW:SC_NORMW + 1], RS.rearrange("p (h t) -> p h t", h=4), ALU.mult, ALU.mult),
                         reads=[k_("os"), k_("rs"), sck], writes=["ob%d" % j])

            active = []
            nxt = [0]

            def start_pair():
                i_ = nxt[0]
                nxt[0] += 1
                active.append([gla_tile(1, order_b[i_], Tb), 0])
                active.append([gla_tile(0, order_f[i_], Tf), 0])

            start_pair()
            while active:
                for ent in list(active):
                    try:
                        ent[1] = next(ent[0])
                    except StopIteration:
                        active.remove(ent)
                if nxt[0] < 18 and (not active or active[-1][1] >= 2) and len(active) <= 2:
                    start_pair()
            P.barrier()
            P.op("dve", I_memset(Tf["S"], 0.0), writes=alias_keys + ["S0_0"])
            if l == 0:
                tap('yapre', OB, ['ob%d' % j for j in range(18)])
            phase('gla_%d' % l)
            TMP.reset()
            SIL = [TMP.alloc([128, 512], BF16) for _ in range(2)]
            vR, srcR = wsl(l, 'w_in', O_RG, 512)
            slotR, kR = ws.get([(vR, srcR)])
            WRg = vR(slotR)
            qq = 0
            for h in range(4):
                for (t0, n) in tts:
                    b = nb()
                    P.op("pe", fm_group(b, lambda kc, h=h: WRg[:, kc, h * 128:(h + 1) * 128], Hk, 8, t0, n), reads=[kR] + hkeys(t0, n), writes=[PSK[b]])
                    P.op("act", I_act(SIL[qq][:, 0:n], PS[b][:, 0:n], AF.Silu), writes=[PSK[b], "sil%d" % qq])
                    P.op("dve", I_tt(OB[:, h, t0:t0 + n], OB[:, h, t0:t0 + n], SIL[qq][:, 0:n], ALU.mult), reads=["sil%d" % qq], writes=["ya"])
                    qq ^= 1
            P.barrier()
            tap("ya%d" % l, OB, ["ya"])

            phase('rg_%d' % l)
            YB = AR.alloc([128, 4, NT], BF16, at=G0)
            YC = AR.alloc([128, 4, NT], BF16, at=G0 + 18432)
            X0 = G0 + 36864
            U = AR.alloc([128, NT], BF16, at=X0)
            V = AR.alloc([128, NT], F32, at=X0 + 4608)
            CB = AR.alloc([128, NT], BF16, at=X0 + 13824)
            TMP.reset()
            CHS = [TMP.alloc([128, 512], F32) for _ in range(2)]
            segs = [(C, NT)] if last else [(0, C), (C, NT)]
            a0 = C if last else 0
            qq = 0
            for jc in range(4):
                parts = []
                for i3, o in enumerate((O_CH, O_CB, O_CC)):
                    src = dr['w_in'][l, :, o + jc * 128:o + (jc + 1) * 128].rearrange("(kc p) c -> p kc c", p=128)
                    parts.append(((lambda s, i3=i3: s[:, 0:3072].rearrange("p (kc b c) -> p kc b c", kc=8, b=3)[:, :, i3, :]), src))
                slot, sk = ws.get(parts)
                W3 = slot[:, 0:3072].rearrange("p (kc b c) -> p kc b c", kc=8, b=3)
                for (t0, n) in tts:
                    bh, bb_, bc = nb(), nb(), nb()
                    for i3, bx in enumerate((bh, bb_, bc)):
                        P.op("pe", fm_group(bx, lambda kc, i3=i3: W3[:, kc, i3, :], Hk, 8, t0, n), reads=[sk] + hkeys(t0, n), writes=[PSK[bx]])
                    P.op("act", I_act(CHS[qq][:, 0:n], PS[bh][:, 0:n], AF.Copy), writes=[PSK[bh], "chs%d" % qq])
                    P.op("dve", I_tt(U[:, t0:t0 + n], PS[bc][:, 0:n], CHS[qq][:, 0:n], ALU.mult), reads=["chs%d" % qq], writes=[PSK[bc], "U"])
                    P.op("act", I_act(CB[:, t0:t0 + n], PS[bb_][:, 0:n], AF.Copy), writes=[PSK[bb_], "CB"])
                    qq ^= 1
                w0 = SC[:, l, SC_CONV + 0 * 4 + jc:SC_CONV + 0 * 4 + jc + 1]
                w1 = SC[:, l, SC_CONV + 1 * 4 + jc:SC_CONV + 1 * 4 + jc + 1]
                w2 = SC[:, l, SC_CONV + 2 * 4 + jc:SC_CONV + 2 * 4 + jc + 1]
                P.op("dve", I_ts(V[:, a0:NT], U[:, a0:NT], w1, None, ALU.mult), reads=["U", sck], writes=["V"])
                for (s0, s1) in segs:
                    P.op("dve", I_stt(V[:, s0 + 1:s1], U[:, s0:s1 - 1], w0, V[:, s0 + 1:s1], ALU.mult, ALU.add), reads=["U", sck], writes=["V"])
                    P.op("dve", I_stt(V[:, s0:s1 - 1], U[:, s0 + 1:s1], w2, V[:, s0:s1 - 1], ALU.mult, ALU.add), reads=["U", sck], writes=["V"])
                P.op("dve", I_tt(YB[:, jc, a0:NT], CB[:, a0:NT], V[:, a0:NT], ALU.mult), reads=["CB", "V"], writes=["yb"])
            P.barrier()
            tap("yb%d" % l, YB, ["yb"])

            phase('conv_%d' % l)
            QRe = AR.alloc([128, 2, NT], BF16, at=X0)
            QRo = AR.alloc([128, 2, NT], BF16, at=X0 + 9216)
            TMP.reset()
            NPT = 10
            PT = [TMP.alloc([128, 512], BF16) for i in range(NPT)]
            P.op("dve", I_memset(QRe[64:128, :, :], 0.0), writes=["qr"])
            P.op("dve", I_memset(QRo[0:64, :, :], 0.0), writes=["qr"])
            QRAW = TMP.alloc([128, 512], F32)
            T1 = TMP.alloc([128, 512], F32)
            T2 = TMP.alloc([128, 512], F32)
            DN = TMP.alloc([128, 2, 128], F32)
            RD = TMP.alloc([128, 2, 128], F32)
            pti = 0
            for kv in range(2):
                vQ, srcQ = wsl(l, 'w_in', O_QA + kv * 256, 256)
                slotQ, kQ = ws.get([(vQ, srcQ)])
                WQ = vQ(slotQ)
                for (t0, n) in tts:
                    for c in range(2):
                        b = nb()
                        P.op("pe", fm_group(b, lambda kc, c=c: WQ[:, kc, c * 128:(c + 1) * 128], Hk, 8, t0, n), reads=[kQ] + hkeys(t0, n), writes=[PSK[b]])
                        P.op("act", I_act(QRAW[:, 0:n], PS[b][:, 0:n], AF.Copy, scale=0.125), writes=[PSK[b], "qraw"])
                        b2 = nb()
                        P.op("pe", I_mms([(PS[b2][:, 0:n], PERM[:, 0, :], QRAW[:, 0:n], True, True)]), reads=["qraw", "c_perm"], writes=[PSK[b2]])
                        P.op("dve", I_tt(T1[:, 0:n], QRAW[:, 0:n], ROPE[:, 0, t0:t0 + n], ALU.mult), reads=["qraw", "c_rope"], writes=["t1"])
                        P.op("dve", I_tt(T2[:, 0:n], PS[b2][:, 0:n], ROPE[:, 1, t0:t0 + n], ALU.mult), reads=["c_rope"], writes=[PSK[b2], "t2"])
                        P.op("dve", I_tt(QRe[0:64, c, t0:t0 + n], T1[0:64, 0:n], T2[0:64, 0:n], ALU.add), reads=["t1", "t2"], writes=["qr"])
                        P.op("dve", I_tt(QRo[64:128, c, t0:t0 + n], T1[64:128, 0:n], T2[64:128, 0:n], ALU.add), reads=["t1", "t2"], writes=["qr"])
                blocks = ([] if last else [0, 1]) + list(range(2, 18))
                def st_phase(jb):
                    nonlocal pti
                    qs = slice(jb * 128, (jb + 1) * 128)
                    if jb < 2:
                        kts = [(0, None), (1, None)]
                    else:
                        kts = []
                        if jb > 2:
                            kts.append((jb - 1, 2))
                        kts.append((jb, None))
                        if jb < 17:
                            kts.append((jb + 1, 3))
                        kts += [(0, None), (1, None)]
                    used = []
                    for (kt, mk) in kts:
                        ks = slice(kt * 128, (kt + 1) * 128)
                        b = nb()
                        P.op("pe", I_mms([(PS[b][:, 0:256], KR[:, kv, ks], QRe[:, :, qs], True, True),
                                          (PS[b][:, 256:512], KR[:, kv, ks], QRo[:, :, qs], True, True)]),
                             reads=["kr", "qr"], writes=[PSK[b]])
                        pk = "pt%d" % pti
                        P.op("act", I_act(PT[pti], PS[b][:, :], AF.Exp), writes=[PSK[b], pk])
                        if mk is not None:
                            P.op("dve", I_tt(PT[pti].rearrange("p (g t) -> p g t", g=4), PT[pti].rearrange("p (g t) -> p g t", g=4),
                                             MSK[:, mk, :].unsqueeze(1).broadcast_to([128, 4, 128]), ALU.mult), reads=["c_msk"], writes=[pk])
                        used.append((kt, pti))
                        pti = (pti + 1) % NPT
                    return used

                def pv_phase(jb, used):
                    qs = slice(jb * 128, (jb + 1) * 128)
                    bo, bd = nb(), nb()
                    lo, ld = [], []
                    nk = len(used)
                    for ii, (kt, pi) in enumerate(used):
                        lo.append((PS[bo][:, 0:256], VAE[:, kt, kv, :], PT[pi][:, 0:256], ii == 0, False))
                        lo.append((PS[bo][:, 0:256], VAO[:, kt, kv, :], PT[pi][:, 256:512], False, ii == nk - 1))
                        ld.append((PS[bd][:, 0:256], ONEEO[:, 0, :], PT[pi][:, 0:256], ii == 0, False))
                        ld.append((PS[bd][:, 0:256], ONEEO[:, 1, :], PT[pi][:, 256:512], False, ii == nk - 1))
                    pks = ["pt%d" % pi for (_, pi) in used]
                    P.op("pe", I_mms(lo), reads=["vae", "vao"] + pks, writes=[PSK[bo]])
                    P.op("pe", I_mms(ld), reads=["c_oneeo"] + pks, writes=[PSK[bd]])
                    P.op("dve", I_tt(DN, PS[bd][:, 0:256].rearrange("p (c t) -> p c t", c=2),
                                     SINKE[:, kv, :].unsqueeze(2).broadcast_to([128, 2, 128]), ALU.add), reads=["sinke"], writes=[PSK[bd], "dn"])
                    P.op("dve", I_recip(RD, DN), reads=["dn"], writes=["rd"])
                    P.op("dve", I_tt(YC[:, kv * 2:kv * 2 + 2, qs], PS[bo][:, 0:256].rearrange("p (c t) -> p c t", c=2), RD, ALU.mult),
                         reads=["rd"], writes=[PSK[bo], "yc"])

                prev_ = None
                for jb in blocks:
                    used_ = st_phase(jb)
                    if prev_ is not None:
                        pv_phase(*prev_)
                    prev_ = (jb, used_)
                pv_phase(*prev_)
            P.barrier()
            tap("yc%d" % l, YC, ["yc"])

            phase('attn_%d' % l)
            Mch = [AR.alloc([128, NT], BF16, at=18432 + i * 4608) for i in range(6)] + \
                  [AR.alloc([128, NT], BF16, at=X0 + i * 4608) for i in range(2)]
            TMP.reset()
            SG = [TMP.alloc([128, 512], F32) for _ in range(3)]
            TT_ = [TMP.alloc([128, 512], F32) for _ in range(3)]
            Ybr = [OB, YB, YC]
            for i in range(8):
                pg, pw = [], []
                for br in range(3):
                    srcg = dr['w_in'][l, :, O_MA + br * 1024 + i * 128:O_MA + br * 1024 + (i + 1) * 128].rearrange("(kc p) c -> p kc c", p=128)
                    pg.append(((lambda s, br=br: s[:, 0:3072].rearrange("p (kc b c) -> p kc b c", kc=8, b=3)[:, :, br, :]), srcg))
                    wn = ('w_branch_a', 'w_branch_b', 'w_branch_c')[br]
                    srcw = dr[wn][l, :, i * 128:(i + 1) * 128].rearrange("(kc p) c -> p kc c", p=128)
                    pw.append(((lambda s, br=br: s[:, 0:1536].rearrange("p (kc b c) -> p kc b c", kc=4, b=3)[:, :, br, :]), srcw))
                slotg, kg_ = ws.get(pg)
                slotw, kw_ = ws.get(pw, keep=1)
                WG = slotg[:, 0:3072].rearrange("p (kc b c) -> p kc b c", kc=8, b=3)
                WBr = slotw[:, 0:1536].rearrange("p (kc b c) -> p kc b c", kc=4, b=3)
                for (t0, n) in tts:
                    bg = [nb(), nb(), nb()]
                    bp = [nb(), nb(), nb()]
                    for br in range(3):
                        P.op("pe", fm_group(bg[br], lambda kc, br=br: WG[:, kc, br, :], Hk, 8, t0, n), reads=[kg_] + hkeys(t0, n), writes=[PSK[bg[br]]])
                        P.op("pe", fm_group(bp[br], lambda kc, br=br: WBr[:, kc, br, :], lambda kc, br=br: Ybr[br][:, kc, :], 4, t0, n),
                             reads=[kw_, ("ya", "yb", "yc")[br]], writes=[PSK[bp[br]]])
                        P.op("act", I_act(SG[br][:, 0:n], PS[bg[br]][:, 0:n], AF.Sigmoid), writes=[PSK[bg[br]], "sg%d" % br])
                        P.op("dve", I_tt(TT_[br][:, 0:n], PS[bp[br]][:, 0:n], SG[br][:, 0:n], ALU.mult), reads=["sg%d" % br], writes=[PSK[bp[br]], "tt%d" % br])
                    P.op("dve", I_tt(TT_[0][:, 0:n], TT_[0][:, 0:n], TT_[1][:, 0:n], ALU.add), reads=["tt1"], writes=["tt0"])
                    P.op("dve", I_tt(Mch[i][:, t0:t0 + n], TT_[0][:, 0:n], TT_[2][:, 0:n], ALU.add), reads=["tt0", "tt2"], writes=["m%d" % i])
            P.barrier()
            tap("m%d" % l, Mch[0], ["m0"])

            phase('merge_%d' % l)
            def add_pass(wparts_fn, nk, src_kc, src_keys, gcol):
                TMP.reset()
                NB_ = 4
                XJ = [TMP.alloc([128, 512], F32) for _ in range(NB_)]
                RO = [TMP.alloc([128, 512], F32) for _ in range(NB_)]
                items = [(j, t0, n) for j in range(8) for (t0, n) in tts]

                def load(it):
                    j, t0, n = items[it]
                    q_ = it % NB_
                    P.dma("sp", I_dma(XJ[q_][:, 0:n], XA[j, :, t0:t0 + n]), reads=["XA_%d_%d" % (j, t0)], writes=["xj%d" % q_], sem_key="xj%d" % q_)

                for it in range(min(NB_ - 1, len(items))):
                    load(it)
                Wj, wkeys, jcur = None, None, -1
                for it, (j, t0, n) in enumerate(items):
                    if j != jcur:
                        Wj, wkeys = wparts_fn(j)
                        jcur = j
                    who = 1 if t0 < C else 0
                    q_ = it % NB_
                    b = nb()
                    P.op("pe", fm_group(b, Wj, src_kc, nk, t0, n), reads=wkeys + src_keys, writes=[PSK[b]])
                    P.op("dve", I_stt(RO[q_][:, 0:n], PS[b][:, 0:n], MOD[:, l, who, gcol * 8 + j:gcol * 8 + j + 1], XJ[q_][:, 0:n], ALU.mult, ALU.add),
                         reads=["xj%d" % q_, "mod%d" % l], writes=[PSK[b], "ro%d" % q_])
                    P.dma("act", I_dma(XA[j, :, t0:t0 + n], RO[q_][:, 0:n]), reads=["ro%d" % q_], writes=["XA_%d_%d" % (j, t0)], sem_key="ro%d" % q_)
                    if it + NB_ - 1 < len(items):
                        load(it + NB_ - 1)
                P.barrier()

            def ln_pass(aw, ab, hw, hb, final=False):
                AR.reset()
                TMP.reset()
                RSg = [AR.alloc([128, 8, 512], F32) for _ in range(2)]
                XO = [AR.alloc([128, 8, 512], F32) for _ in range(2)]
                SQl = [AR.alloc([128, 512], F32) for _ in range(2)]
                XH = [AR.alloc([128, 512], F32) for _ in range(4)]
                OT = [AR.alloc([128, 1024], F32) for _ in range(2)]
                MEANs = [TMP.alloc([128, 512], F32) for _ in range(2)]
                MSQs = [TMP.alloc([128, 512], F32) for _ in range(2)]
                VARs = [TMP.alloc([128, 512], F32) for _ in range(2)]
                RSTDs = [TMP.alloc([128, 512], F32) for _ in range(2)]
                tl = [(t0, n) for (t0, n) in tts if not (final and t0 < C)]
                st = dict(sq=0, xh=0, ot=0)
                banks = {}

                def load(i):
                    t0, n = tl[i]
                    q_ = i % 2
                    P.dma("sp", I_dma(RSg[q_][:, :, 0:n], XA[:, :, t0:t0 + n].rearrange("kc p t -> p kc t")),
                          reads=["XA_%d_%d" % (j, t0) for j in range(8)], writes=["rsg%d" % q_], sem_key="rsg%d" % q_)

                def stats_pe(i):
                    t0, n = tl[i]
                    q_ = i % 2
                    rk = "rsg%d" % q_
                    b1, b2 = nb(), nb()
                    banks[i] = (b1, b2)
                    reserved.add(b1)
                    reserved.add(b2)
                    P.op("pe", I_mms([(PS[b1][:, 0:n], ONES, RSg[q_][:, kc, 0:n], kc == 0, kc == 7) for kc in range(8)]), reads=[rk, "c_ones"], writes=[PSK[b1]])
                    for kc in range(8):
                        si = st['sq']
                        P.op("act", I_act(SQl[si][:, 0:n], RSg[q_][:, kc, 0:n], AF.Square), reads=[rk], writes=["sql%d" % si])
                        P.op("pe", I_mms([(PS[b2][:, 0:n], ONES, SQl[si][:, 0:n], kc == 0, kc == 7)]), reads=["sql%d" % si, "c_ones"], writes=[PSK[b2]])
                        st['sq'] = si ^ 1

                def stats_dve(i):
                    t0, n = tl[i]
                    q_ = i % 2
                    b1, b2 = banks[i]
                    MEAN, MSQ, VAR, RSTD = MEANs[q_], MSQs[q_], VARs[q_], RSTDs[q_]
                    kmean, kmsq, kvar, krstd = "mean%d" % q_, "msq%d" % q_, "var%d" % q_, "rstd%d" % q_
                    P.op("dve", I_ts(MEAN[:, 0:n], PS[b1][:, 0:n], 1.0 / D, None, ALU.mult), writes=[PSK[b1], kmean])
                    P.op("dve", I_tt(MSQ[:, 0:n], MEAN[:, 0:n], MEAN[:, 0:n], ALU.mult), reads=[kmean], writes=[kmsq])
                    P.op("dve", I_stt(VAR[:, 0:n], PS[b2][:, 0:n], 1.0 / D, MSQ[:, 0:n], ALU.mult, ALU.subtract), reads=[kmsq], writes=[PSK[b2], kvar])
                    P.op("act", I_act(VAR[:, 0:n], VAR[:, 0:n], AF.Sqrt, bias=LN_EPS), writes=[kvar])
                    P.op("dve", I_recip(RSTD[:, 0:n], VAR[:, 0:n]), reads=[kvar], writes=[krstd])
                    reserved.discard(b1)
                    reserved.discard(b2)

                def norm(i):
                    t0, n = tl[i]
                    who = 1 if t0 < C else 0
                    q_ = i % 2
                    rk, xk = "rsg%d" % q_, "xo%d" % q_
                    MEAN, RSTD = MEANs[q_], RSTDs[q_]
                    kmean, krstd = "mean%d" % q_, "rstd%d" % q_
                    for kc in range(8):
                        xi = st['xh']
                        xhk = "xh%d" % xi
                        eng_ = "dve"
                        P.op(eng_, I_tt(XH[xi][:, 0:n], RSg[q_][:, kc, 0:n], MEAN[:, 0:n], ALU.subtract), reads=[rk, kmean], writes=[xhk])
                        P.op(eng_, I_tt(XH[xi][:, 0:n], XH[xi][:, 0:n], RSTD[:, 0:n], ALU.mult), reads=[krstd], writes=[xhk])
                        P.op(eng_, I_ts(XO[q_][:, kc, 0:n], XH[xi][:, 0:n], SC[:, l, aw + kc:aw + kc + 1], SC[:, l, ab + kc:ab + kc + 1], ALU.mult, ALU.add),
                             reads=[xhk, sck], writes=[xk])
                        if not final:
                            P.op("act", I_act(Ht[:, kc, t0:t0 + n], XH[xi][:, 0:n], AF.Identity,
                                              bias=SC[:, l, hb + 8 * who + kc:hb + 8 * who + kc + 1], scale=SC[:, l, hw + 8 * who + kc:hw + 8 * who + kc + 1]),
                                 reads=[xhk, sck], writes=hkeys(t0, n))
                        st['xh'] = (xi + 1) % 4
                    if not final:
                        P.dma("sp", I_dma(XA[:, :, t0:t0 + n].rearrange("kc p t -> p kc t"), XO[q_][:, :, 0:n]), reads=[xk],
                              writes=["XA_%d_%d" % (j, t0) for j in range(8)], sem_key=xk)
                    else:
                        for s_ in range(n // 128):
                            bs2 = [nb(), nb()]
                            for hb_ in range(2):
                                P.op("pe", I_mms([(PS[bs2[hb_]][:, c * 128:(c + 1) * 128], XO[q_][:, hb_ * 4 + c, s_ * 128:(s_ + 1) * 128], IDENT, True, True)
                                                  for c in range(4)]), reads=[xk, "c_ident"], writes=[PSK[bs2[hb_]]])
                            oi = st['ot']
                            ok = "ot%d" % oi
                            P.op("act", I_act(OT[oi][:, 0:512], PS[bs2[0]][:, :], AF.Copy), writes=[PSK[bs2[0]], ok])
                            P.op("dve", I_copy(OT[oi][:, 512:1024], PS[bs2[1]][:, :]), writes=[PSK[bs2[1]], ok])
                            r0 = t0 - C + s_ * 128
                            P.dma("sp", I_dma(out[r0:r0 + 128, :], OT[oi]), reads=[ok], sem_key=ok)
                            st['ot'] = oi ^ 1

                nt_ = len(tl)
                load(0)
                if nt_ > 1:
                    load(1)
                stats_pe(0)
                stats_dve(0)
                for i in range(nt_):
                    if i + 1 < nt_:
                        stats_pe(i + 1)
                    norm(i)
                    if i + 1 < nt_:
                        stats_dve(i + 1)
                    if i + 2 < nt_:
                        load(i + 2)
                P.barrier(engs=("pe", "act", "dve", "sp", "pool"))

            def wout_parts(j, cache={}):
                hf = j // 4
                if hf not in cache:
                    vO, srcO = wsl(l, 'w_out', hf * 512, 512)
                    slotO, kO = ws.get([(vO, srcO)])
                    cache.clear()
                    cache[hf] = (vO(slotO), kO)
                WO, kO = cache[hf]
                return (lambda kc, j=j: WO[:, kc, (j % 4) * 128:(j % 4 + 1) * 128]), [kO]

            add_pass(wout_parts, 8, lambda kc: Mch[kc], ["m%d" % i for i in range(8)], 2)
            ln_pass(SC_AW1, SC_AB1, SC_HW2, SC_HB2)

            phase('ln1_%d' % l)
            AR.reset()
            TMP.reset()
            Ach = [AR.alloc([128, NT], BF16) for _ in range(NFF)]
            UG = TMP.alloc([128, NT], BF16)
            UV = TMP.alloc([128, NT], BF16)
            VG = TMP.alloc([128, NT], BF16)
            VV = TMP.alloc([128, NT], BF16)
            PTMP = TMP.alloc([128, 512], BF16)
            for g in range(NFF // 2):
                parts = []
                for i2, o in enumerate((0, DFF)):
                    src = dr['ffn_up'][l, :, o + g * 256:o + (g + 1) * 256].rearrange("(kc p) c -> p kc c", p=128)
                    parts.append(((lambda s, i2=i2: s[:, 0:4096].rearrange("p (kc b c) -> p kc b c", kc=8, b=2)[:, :, i2, :]), src))
                slot, sk = ws.get(parts)
                WU = slot[:, 0:4096].rearrange("p (kc b c) -> p kc b c", kc=8, b=2)
                for c in range(2):
                    jf = g * 2 + c
                    cg, cv_ = jf, NFF + jf
                    wg = [SC[:, l, SC_FCONV + k * 44 + cg:SC_FCONV + k * 44 + cg + 1] for k in range(3)]
                    wv = [SC[:, l, SC_FCONV + k * 44 + cv_:SC_FCONV + k * 44 + cv_ + 1] for k in range(3)]

                    def seg_of(t0):
                        return (0, C) if t0 < C else (C, NT)

                    def conv_tile(ti):
                        t0, n = tts[ti]
                        s0, s1 = seg_of(t0)
                        lo_ = t0 if t0 > s0 else s0 + 1
                        hi_ = t0 + n if t0 + n < s1 else s1 - 1
                        nbr = ["%d" % x for x in (ti - 1, ti, ti + 1) if 0 <= x < len(tts)]
                        for (Ub, Vb, un, vn, wt, on_pool) in ((UG, VG, "ug", "vg", wg, False), (UV, VV, "uv", "vv", wv, True)):
                            P.op("dve", I_stt(Vb[:, lo_:t0 + n], Ub[:, lo_ - 1:t0 + n - 1], wt[0], Vb[:, lo_:t0 + n], ALU.mult, ALU.add),
                                 reads=[un + x for x in nbr] + [sck], writes=[vn + "%d" % ti])
                            if False:
                                m_ = hi_ - t0
                                P.op("pool", I_ts(PTMP[:, 0:m_], Ub[:, t0 + 1:hi_ + 1], wt[2], None, ALU.mult), reads=[un + x for x in nbr] + [sck], writes=["ptmp"])
                                P.op("pool", I_tt(Vb[:, t0:hi_], Vb[:, t0:hi_], PTMP[:, 0:m_], ALU.add), reads=["ptmp"], writes=[vn + "%d" % ti])
                            else:
                                P.op("dve", I_stt(Vb[:, t0:hi_], Ub[:, t0 + 1:hi_ + 1], wt[2], Vb[:, t0:hi_], ALU.mult, ALU.add),
                                     reads=[un + x for x in nbr] + [sck], writes=[vn + "%d" % ti])
                        P.op("act", I_act(Ach[jf][:, t0:t0 + n], VG[:, t0:t0 + n], AF.Silu), reads=["vg%d" % ti], writes=["a%d" % jf])
                        P.op("dve", I_tt(Ach[jf][:, t0:t0 + n], Ach[jf][:, t0:t0 + n], VV[:, t0:t0 + n], ALU.mult), reads=["vv%d" % ti], writes=["a%d" % jf])

                    pending = []
                    for ti, (t0, n) in enumerate(tts):
                        b1, b2 = nb(), nb()
                        P.op("pe", fm_group(b1, lambda kc, c=c: WU[:, kc, 0, c * 128:(c + 1) * 128], Hk, 8, t0, n), reads=[sk] + hkeys(t0, n), writes=[PSK[b1]])
                        P.op("pe", fm_group(b2, lambda kc, c=c: WU[:, kc, 1, c * 128:(c + 1) * 128], Hk, 8, t0, n), reads=[sk] + hkeys(t0, n), writes=[PSK[b2]])
                        P.op("act", I_act(VG[:, t0:t0 + n], PS[b1][:, 0:n], AF.Identity, scale=wg[1]), reads=[sck], writes=[PSK[b1], "vg%d" % ti])
                        P.op("act", I_act(UG[:, t0:t0 + n], PS[b1][:, 0:n], AF.Copy), writes=[PSK[b1], "ug%d" % ti])
                        P.op("act", I_act(VV[:, t0:t0 + n], PS[b2][:, 0:n], AF.Identity, scale=wv[1]), reads=[sck], writes=[PSK[b2], "vv%d" % ti])
                        P.op("act", I_act(UV[:, t0:t0 + n], PS[b2][:, 0:n], AF.Copy), writes=[PSK[b2], "uv%d" % ti])
                        pending.append(ti)
                        ready = []
                        for pt_ in pending:
                            p0, pn = tts[pt_]
                            if p0 + pn >= seg_of(p0)[1] or pt_ < ti:
                                ready.append(pt_)
                        for pt_ in ready:
                            pending.remove(pt_)
                            conv_tile(pt_)
                    for pt_ in pending:
                        conv_tile(pt_)
            P.barrier(engs=("pe", "act", "dve", "sp", "pool"))
            tap("a%d" % l, Ach[0], ["a0"])

            phase('ffnup_%d' % l)
            def wdown_parts(j):
                src = dr['ffn_down'][l, :, j * 128:(j + 1) * 128].rearrange("(kc p) c -> p kc c", p=128)
                vf = (lambda s: s[:, 0:NFF * 128].rearrange("p (kc c) -> p kc c", kc=NFF))
                slot, sk = ws.get([(vf, src)])
                Wd = vf(slot)
                return (lambda kc: Wd[:, kc, :]), [sk]

            add_pass(wdown_parts, NFF, lambda kc: Ach[kc], ["a%d" % i for i in range(NFF)], 5)
            if last:
                ln_pass(SC_LN2W, SC_LN2B, 0, 0, final=True)
            elif l == n_layers - 1:
                ln_pass(SC_LN2W, SC_LN2B, 0, 0, final=True)
            else:
                ln_pass(SC_AW2, SC_AB2, SC_HW1N, SC_HB1N)

        P.final_wait()
        return ws

    Pd = Prog(nc, dry=True)
    wsd = build(Pd)
    global ALLW
    ALLW = wsd.descs
    P = Prog(nc, dry=False)
    build(P)
    print("ops:", {e: len(P.ops[e]) for e in ENGS}, "weight loads:", len(ALLW))
    stack = ExitStack()
    P.emit(stack)
    stack.close()
    return nc


ALLW = []
_CACHE = {}


def kernel(**inputs):
    x = np.ascontiguousarray(inputs['x'], dtype=np.float32)
    c = np.asarray(inputs['c'], dtype=np.float32)
    ctx = np.ascontiguousarray(inputs['ctx'], dtype=np.float32)
    c_ctx = np.asarray(inputs['c_ctx'], dtype=np.float32)
    consts = make_consts()
    if 'nc' not in _CACHE:
        _CACHE['nc'] = build_program()
    nc = _CACHE['nc']
    shared = {k: np.ascontiguousarray(inputs[k], dtype=np.float32) for k in WEIGHT_SHAPES}
    shared.update(consts)
    in_maps = []
    for b in range(8):
        m = dict(shared)
        m['x'] = x[b]
        m['ctx'] = ctx[b]
        m['cvec'] = np.ascontiguousarray(np.stack([c[b], c_ctx], 0))
        in_maps.append(m)
    res = run_bass_kernel_spmd(nc, in_maps, core_ids=list(range(8)))
    return np.stack([np.asarray(r['out'], dtype=np.float32) for r in res.results], 0)
```

```python
import numpy as np
from contextlib import ExitStack
import concourse.bass as bass
import concourse.mybir as mybir
from concourse.bass_utils import run_bass_kernel_spmd

F32 = mybir.dt.float32
BF16 = mybir.dt.bfloat16
U8 = mybir.dt.uint8
AF = mybir.ActivationFunctionType
ALU = mybir.AluOpType

ENGS = ("pe", "act", "dve", "pool", "sp")

D = 1024
T = 2048
C = 256
NT = T + C
DEPTH = 4
KC = 8
DFF = 2816
NFF = DFF // 128
IN_COLS = 6944
ALPHA = (2 * DEPTH) ** 0.25
LN_EPS = 1e-5
RMS_EPS = 1e-6
TTS = [(0, 256), (256, 512), (768, 512), (1280, 512), (1792, 512)]
O_KG, O_VG, O_GF, O_GB, O_KA, O_VA = 0, 256, 768, 784, 800, 928
O_QG, O_RG, O_CH, O_CB, O_CC, O_QA, O_MA = 1056, 1312, 1824, 2336, 2848, 3360, 3872


class Prog:
    def __init__(self, nc, dry=False):
        self.nc = nc
        self.dry = dry
        self.ops = {e: [] for e in ENGS}
        self.last_w = {}
        self.readers = {}
        self.dma_sems = {}
        self.all_dma_tokens = {}

    def _deps_for(self, reads, writes):
        deps = []
        for k in reads:
            t = self.last_w.get(k)
            if t is not None:
                deps.append(t)
        for k in writes:
            t = self.last_w.get(k)
            if t is not None:
                deps.append(t)
            deps.extend(self.readers.get(k, ()))
        out, seen = [], set()
        for t in deps:
            if t not in seen:
                seen.add(t)
                out.append(t)
        return out

    def _update(self, tok, reads, writes):
        for k in reads:
            self.readers.setdefault(k, []).append(tok)
        for k in writes:
            self.last_w[k] = tok
            self.readers[k] = []

    def op(self, eng, fn, reads=(), writes=()):
        if self.dry:
            return
        idx = len(self.ops[eng])
        tok = ('E', eng, idx)
        fdeps = []
        for t in self._deps_for(reads, writes):
            if t[0] == 'E' and t[1] == eng:
                if eng == 'pe':
                    continue
            fdeps.append(t)
        self.ops[eng].append(dict(fn=fn, deps=fdeps, inc=False, kind='op'))
        self._update(tok, reads, writes)

    def dma(self, eng, fn, reads=(), writes=(), sem_key=None):
        if self.dry:
            return
        ent = self.dma_sems.setdefault(sem_key, [None, 0])
        ent[1] += 16
        tok = ('D', sem_key, ent[1])
        deps = [t for t in self._deps_for(reads, writes) if not (t[0] == 'D' and t[1] == sem_key)]
        self.ops[eng].append(dict(fn=fn, deps=deps, inc=False, kind='dma', sem_key=sem_key))
        self._update(tok, reads, writes)
        self.all_dma_tokens[sem_key] = tok

    def barrier(self, engs=("pe", "act", "dve", "sp")):
        if self.dry:
            return
        toks = [('E', e, len(self.ops[e]) - 1) for e in engs if self.ops[e]]
        dtoks = list(self.all_dma_tokens.values())
        for e in engs:
            deps = list(toks) + dtoks
            self.ops[e].append(dict(fn=None, deps=deps, inc=False, kind='wait'))

    def final_wait(self, eng="sp"):
        if self.dry:
            return
        deps = list(self.all_dma_tokens.values())
        for e in ENGS:
            if e != eng and self.ops[e]:
                deps.append(('E', e, len(self.ops[e]) - 1))
        self.ops[eng].append(dict(fn=None, deps=deps, inc=False, kind='wait'))

    def emit(self, stack):
        nc = self.nc

        def resolve(t):
            _, e, i = t
            while i >= 0 and self.ops[e][i]['kind'] != 'op':
                i -= 1
            return (e, i)

        for e in ENGS:
            for o in self.ops[e]:
                nd = []
                for t in o['deps']:
                    if t[0] == 'E':
                        e2, i2 = resolve(t)
                        if i2 < 0:
                            continue
                        self.ops[e2][i2]['inc'] = True
                        nd.append(('E', e2, i2))
                    else:
                        nd.append(t)
                o['deps'] = nd
        rank = {}
        for e in ENGS:
            r = 0
            for i, o in enumerate(self.ops[e]):
                if o['inc']:
                    r += 1
                    rank[(e, i)] = r
            assert r < 60000, (e, r)
        esem = {e: stack.enter_context(nc.semaphore("s_" + e)) for e in ENGS}
        for n, k in enumerate(self.dma_sems):
            self.dma_sems[k][0] = stack.enter_context(nc.semaphore("d%d" % n))
        block = stack.enter_context(nc.Block())
        prog = self

        def run(e, engine):
            waited = {}
            for o in prog.ops[e]:
                for t in o['deps']:
                    if t[0] == 'E':
                        sem, val, key = esem[t[1]], rank[(t[1], t[2])], ('E', t[1])
                    else:
                        sem, val, key = prog.dma_sems[t[1]][0], t[2], ('D', t[1])
                    if waited.get(key, 0) >= val:
                        continue
                    waited[key] = val
                    engine.wait_ge(sem, val)
                if o['fn'] is None:
                    continue
                ins = o['fn'](engine)
                if o['kind'] == 'dma':
                    ins.then_inc(prog.dma_sems[o['sem_key']][0], 16)
                elif o['inc']:
                    ins.then_inc(esem[e], 1)

        @block.tensor
        def _(eng):
            run("pe", eng)

        @block.scalar
        def _(eng):
            run("act", eng)

        @block.vector
        def _(eng):
            run("dve", eng)

        @block.gpsimd
        def _(eng):
            run("pool", eng)

        @block.sync
        def _(eng):
            run("sp", eng)


def I_mms(lst):
    def f(e):
        r = None
        for (o, l, rr, s, t) in lst:
            r = e.matmul(o, l, rr, start=s, stop=t)
        return r
    return f


def I_act(out, in_, func, bias=None, scale=None):
    kw = {}
    if bias is not None:
        kw['bias'] = bias
    if scale is not None:
        kw['scale'] = scale
    return lambda e: e.activation(out, in_, func, **kw)


def I_tt(out, a, b, op):
    return lambda e: e.tensor_tensor(out, a, b, op)


def I_ts(out, a, s1, s2, op0, op1=None):
    if op1 is None:
        return lambda e: e.tensor_scalar(out, a, s1, None, op0)
    return lambda e: e.tensor_scalar(out, a, s1, s2, op0, op1)


def I_stt(out, in0, scalar, in1, op0, op1):
    return lambda e: e.scalar_tensor_tensor(out, in0, scalar, in1, op0, op1)


def I_copy(out, in_):
    return lambda e: e.tensor_copy(out, in_)


def I_memset(ap, v):
    return lambda e: e.memset(ap, v)


def I_recip(out, in_):
    return lambda e: e.reciprocal(out, in_)


def I_dma(out, in_):
    return lambda e: e.dma_start(out=out, in_=in_)


def make_consts():
    m = np.arange(128)[:, None]
    l = np.arange(128)[None, :]
    same = (m // 64) == (l // 64)
    tri = -np.stack([same & (m <= l), same & (m >= l), same & (m > l), same & (m < l)], 1).astype(np.float32) / 16.0
    msk = np.stack([same & (l >= m), same & (l <= m), l <= m, m <= l], 1).astype(np.float32)
    perm = np.zeros((128, 5, 128), np.float32)
    sp = np.zeros((128, 128), np.float32)
    for pp in range(128):
        d = pp % 64
        i = d % 32
        if i < 16:
            sp[pp + 16, pp] = -1.0
        else:
            sp[pp - 16, pp] = 1.0
    perm[:, 0, :] = sp
    for kv in range(2):
        for pp in range(128):
            src = kv * 64 + pp % 64
            perm[src, 1 + kv, pp] = 1.0
            perm[:, 3 + kv, pp] = sp[:, src]
    ident = np.eye(128, dtype=np.float32)
    ones = np.ones((128, 128), np.float32)
    oneeo = np.zeros((128, 2, 128), np.float32)
    oneeo[:, 0, 0:64] = 1.0
    oneeo[:, 1, 64:128] = 1.0
    t = np.arange(T)
    row = (t // 64).astype(np.float32)
    col = (t % 64).astype(np.float32)
    inv_freq = (np.float32(10000.0) ** (-np.arange(0, 32, 2, dtype=np.float32) / np.float32(32))).astype(np.float32)
    ang = np.concatenate([row[:, None] * inv_freq[None, :], col[:, None] * inv_freq[None, :]], axis=1).astype(np.float32)
    rope = np.zeros((128, 2, NT), np.float32)
    rope[:, 0, :C] = 1.0
    for pp in range(128):
        d = pp % 64
        a = (d // 32) * 16 + (d % 16)
        rope[pp, 0, C:] = np.cos(ang[:, a])
        rope[pp, 1, C:] = np.sin(ang[:, a])
    return dict(c_tri=tri, c_msk=msk, c_perm=perm, c_ident=ident, c_ones=ones, c_oneeo=oneeo, c_rope=rope)


WEIGHT_SHAPES = dict(
    w_ada=[DEPTH, D, 6 * D], b_ada=[DEPTH, 6 * D], w_in=[DEPTH, D, IN_COLS],
    gla_gate_up_f=[DEPTH, 16, 256], gla_gate_bias_f=[DEPTH, 256],
    gla_gate_up_b=[DEPTH, 16, 256], gla_gate_bias_b=[DEPTH, 256],
    gla_norm_w=[DEPTH, 128], conv_w=[DEPTH, 3, 512], att_sink=[DEPTH, 8],
    w_branch_a=[DEPTH, 512, D], w_branch_b=[DEPTH, 512, D], w_branch_c=[DEPTH, 512, D],
    w_out=[DEPTH, D, D], ln1_w=[DEPTH, D], ln1_b=[DEPTH, D],
    ffn_up=[DEPTH, D, 2 * DFF], ffn_conv=[DEPTH, 3, 2 * DFF], ffn_down=[DEPTH, DFF, D],
    ln2_w=[DEPTH, D], ln2_b=[DEPTH, D],
)
CONST_SHAPES = dict(c_tri=[128, 4, 128], c_msk=[128, 4, 128], c_perm=[128, 5, 128], c_ident=[128, 128],
                    c_ones=[128, 128], c_oneeo=[128, 2, 128], c_rope=[128, 2, NT])

SC_LN1W, SC_LN1B, SC_LN2W, SC_LN2B = 0, 8, 16, 24
SC_CONV = 32
SC_FCONV = 44
SC_NORMW = 176
SC_A1 = 184
SC_AW1, SC_AB1 = 200, 208
SC_HW2, SC_HB2 = 216, 232
SC_AW2, SC_AB2 = 248, 256
SC_HW1N, SC_HB1N = 264, 280
SC_S1 = 296
SC_TMP = 312
SC_N = 320


class StopBuild(Exception):
    pass


def build_program(n_layers=DEPTH, taps=None, stop=None):
    nc = bass.Bass("TRN2", target_bir_lowering=False)
    taps = taps or {}
    dr = {}
    dr['x'] = nc.dram_tensor("x", [T, D], F32, kind="ExternalInput").ap()
    dr['ctx'] = nc.dram_tensor("ctx", [C, D], F32, kind="ExternalInput").ap()
    dr['cvec'] = nc.dram_tensor("cvec", [2, D], F32, kind="ExternalInput").ap()
    for k, s in WEIGHT_SHAPES.items():
        dr[k] = nc.dram_tensor(k, s, F32, kind="ExternalInput").ap()
    for k, s in CONST_SHAPES.items():
        dr[k] = nc.dram_tensor(k, s, F32, kind="ExternalInput").ap()
    out = nc.dram_tensor("out", [T, D], F32, kind="ExternalOutput").ap()
    XA = nc.dram_tensor("xa_scr", [KC, 128, NT], F32, kind="Internal").ap()
    tap_dr = {k: nc.dram_tensor("tap_" + k, list(s), F32, kind="ExternalOutput").ap() for k, s in taps.items()}

    off = [16640]
    OFFS = {}
    uniq = [0]

    def alloc(name, shape, dtype, nbytes=None):
        esz = 4 if dtype == F32 else 2
        n = int(np.prod(shape[1:])) * esz
        if nbytes is None:
            nbytes = n
        nbytes = (nbytes + 63) // 64 * 64
        t = nc.alloc_sbuf_tensor_at(name, list(shape), dtype, offset=off[0])
        OFFS[name] = off[0]
        off[0] += nbytes
        return t[:]

    class Region:
        def __init__(self, name, size):
            self.name, self.base, self.size, self.cur, self.n = name, off[0], size, 0, 0
            off[0] += size

        def reset(self):
            self.cur = 0

        def alloc(self, shape, dtype, at=None):
            esz = 4 if dtype == F32 else 2
            nb = (int(np.prod(shape[1:])) * esz + 63) // 64 * 64
            if at is None:
                at = self.cur
                self.cur += nb
            assert at + nb <= self.size, (self.name, at, nb, self.size)
            self.n += 1
            return nc.alloc_sbuf_tensor_at("%s_%d" % (self.name, self.n), list(shape), dtype, offset=self.base + at)[:]

    IDENT = alloc("ident", [128, 128], F32)
    ONES = alloc("ones", [128, 128], F32)
    TRI = alloc("tri", [128, 4, 128], F32)
    MSK = alloc("msk", [128, 4, 128], BF16)
    PERM = alloc("perm", [128, 5, 128], F32)
    ONEEO = alloc("oneeo", [128, 2, 128], BF16)
    ONEROW = alloc("onerow", [1, 16], F32)
    ROPE = alloc("rope", [128, 2, NT], BF16)
    MOD = alloc("mod", [128, DEPTH, 2, 48], F32)
    SC = alloc("sc", [128, DEPTH, SC_N], F32)
    ROWS = [alloc("row%d" % i, [8, 128], F32) for i in range(8)]
    WUP = alloc("wup", [33, 512], F32)
    SINK = alloc("sink", [128, 16], F32)
    SINKE = alloc("sinke", [128, 2, 2], F32)
    SCB = alloc("scb", [128, 8, 2], BF16)
    CV = alloc("cv", [128, 16], F32)
    Ht = alloc("H", [128, KC, NT], BF16)
    AR = Region("AR", NFF * NT * 2)
    WR = [alloc("wr%d" % i, [128, 4096], BF16) for i in range(3)]
    TMP = Region("TMP", 229376 - off[0])
    print('TMP size', TMP.size)
    assert TMP.size >= 18432, TMP.size
    PS = [nc.alloc_psum_tensor("ps%d" % i, [128, 512], F32)[:] for i in range(8)]
    PSK = ["ps%d" % i for i in range(8)]

    def build(P):
        try:
            return build_inner(P)
        except StopBuild:
            P.barrier()
            P.final_wait()
            return P._ws

    def build_inner(P):
        class WS:
            def __init__(self):
                self.descs = []
                self.next = 0
                self.issued = 0

            def get(self, parts, keep=0):
                if P.dry:
                    self.descs.append(parts)
                    return WR[0], "wr0"
                i = self.next
                self.next += 1
                lim = min(len(ALLW), i + 3 - keep)
                while self.issued < lim:
                    j = self.issued
                    s = j % 3
                    for (dv, src) in ALLW[j]:
                        P.dma("pool", I_dma(dv(WR[s]), src), writes=["wr%d" % s], sem_key="wr%d" % s)
                    self.issued += 1
                return WR[i % 3], "wr%d" % (i % 3)

        ws = WS()
        P._ws = ws
        if not P.dry:
            ws.descs = None

        def wsl(l, name, c0, n):
            src = dr[name][l, :, c0:c0 + n].rearrange("(kc p) c -> p kc c", p=128)
            return (lambda s, n=n: s[:, 0:8 * n].rearrange("p (kc c) -> p kc c", kc=8)), src

        bank = [0]
        reserved = set()

        def nb():
            while True:
                b = bank[0]
                bank[0] = (b + 1) % 8
                if b not in reserved:
                    return b

        def phase(name):
            if stop == name:
                raise StopBuild()

        def tap(name, ap, keys):
            if name in tap_dr:
                P.dma("pool", I_dma(tap_dr[name], ap), reads=keys, sem_key="tap_" + name)

        for nm, dst in (("c_ident", IDENT), ("c_ones", ONES), ("c_tri", TRI), ("c_perm", PERM)):
            P.dma("sp", I_dma(dst, dr[nm]), writes=[nm], sem_key=nm)
        for nm, dst in (("c_msk", MSK), ("c_oneeo", ONEEO), ("c_rope", ROPE)):
            P.dma("pool", I_dma(dst, dr[nm]), writes=[nm], sem_key=nm)
        P.op("dve", I_memset(ONEROW, 1.0), writes=["onerow"])

        rowi = [0]

        def row_to_cols(src_row_ap, n, dst_ap, dkey):
            nchk = n // 128
            b = nb()
            q = rowi[0]
            rowi[0] = (q + 1) % 8
            rk = "row%d" % q
            P.dma("sp", I_dma(ROWS[q][0:nchk, :], src_row_ap.rearrange("o (a b) -> (o a) b", b=128)), writes=[rk], sem_key=rk)
            P.op("pe", I_mms([(PS[b][:, 0:nchk], ROWS[q][0:nchk, :], IDENT[0:nchk, 0:nchk], True, True)]),
                 reads=[rk, "c_ident"], writes=[PSK[b]])
            P.op("dve", I_copy(dst_ap, PS[b][:, 0:nchk]), reads=[], writes=[PSK[b], dkey])

        for w in range(2):
            row_to_cols(dr['cvec'][w:w + 1, :], 1024, CV[:, w * 8:(w + 1) * 8], "cv")
        for w in range(2):
            P.op("act", I_act(SCB[:, :, w], CV[:, w * 8:(w + 1) * 8], AF.Silu), reads=["cv"], writes=["scb"])
        ROW2 = [TMP.alloc([2, 512], F32, at=1024 + i * 2048) for i in range(2)]
        for l in range(n_layers):
            bA = nb()
            reserved.add(bA)
            for s in range(12):
                vf, src = wsl(l, 'w_ada', s * 512, 512)
                slot, sk = ws.get([(vf, src)])
                W = vf(slot)
                bt = nb()
                q2 = s % 2
                P.op("pe", I_mms([(PS[bt][0:2, 0:512], SCB[:, kc, :], W[:, kc, :], kc == 0, kc == 7) for kc in range(8)]),
                     reads=[sk, "scb"], writes=[PSK[bt]])
                P.op("dve", I_copy(ROW2[q2], PS[bt][0:2, 0:512]), writes=[PSK[bt], "rowb%d" % q2])
                P.op("pe", I_mms([(PS[bA][:, 2 * (s * 4 + j):2 * (s * 4 + j) + 2], ROW2[q2][0:2, j * 128:(j + 1) * 128], IDENT[0:2, 0:2], True, True)
                                  for j in range(4)]), reads=["rowb%d" % q2, "c_ident"], writes=[PSK[bA]])
            BT = TMP.alloc([128, 48], F32, at=0)
            for pc in range(6):
                row_to_cols(dr['b_ada'][l:l + 1, pc * 1024:(pc + 1) * 1024], 1024, BT[:, pc * 8:(pc + 1) * 8], "bt")
            pa = PS[bA][:, 0:96].rearrange("p (n w) -> p n w", w=2)
            for w in range(2):
                P.op("dve", I_tt(MOD[:, l, w, :], pa[:, :, w], BT[:, :], ALU.add), reads=["bt"], writes=[PSK[bA], "mod%d" % l])
            reserved.discard(bA)
        P.barrier()

        tap('mod', MOD[:, 0, :, :], ['mod0'])
        phase('ada')
        def sc(l, o, n=8):
            return SC[:, l, o:o + n]

        for l in range(n_layers):
            for nm, o in (("ln1_w", SC_LN1W), ("ln1_b", SC_LN1B), ("ln2_w", SC_LN2W), ("ln2_b", SC_LN2B)):
                row_to_cols(dr[nm][l:l + 1, :], 1024, sc(l, o), "sc%d" % l)
            cw = dr['conv_w'][l].rearrange("k c -> (k c)").unsqueeze(0)
            row_to_cols(cw[:, 0:1024], 1024, SC[:, l, SC_CONV:SC_CONV + 8], "sc%d" % l)
            row_to_cols(cw[:, 1024:1536], 512, SC[:, l, SC_CONV + 8:SC_CONV + 12], "sc%d" % l)
            fw = dr['ffn_conv'][l].rearrange("k c -> (k c)").unsqueeze(0)
            for pc in range(17):
                n = 1024 if pc < 16 else 512
                row_to_cols(fw[:, pc * 1024:pc * 1024 + n], n, SC[:, l, SC_FCONV + pc * 8:SC_FCONV + pc * 8 + n // 128], "sc%d" % l)
            row_to_cols(dr['gla_norm_w'][l:l + 1, :], 128, SC[:, l, SC_NORMW:SC_NORMW + 1], "sc%d" % l)
        for l in range(n_layers):
            k = ["sc%d" % l, "mod%d" % l]
            wk = ["sc%d" % l]
            for w in range(2):
                md = MOD[:, l, w, :]
                P.op("dve", I_ts(sc(l, SC_S1 + 8 * w), md[:, 8:16], 1.0, None, ALU.add), reads=k, writes=wk)
                P.op("dve", I_ts(sc(l, SC_A1 + 8 * w), md[:, 8:16], 1.0, 1.0 / ALPHA, ALU.add, ALU.mult), reads=k, writes=wk)
                P.op("dve", I_ts(sc(l, SC_TMP), md[:, 32:40], 1.0, None, ALU.add), reads=k, writes=wk)
                P.op("dve", I_tt(sc(l, SC_HW2 + 8 * w), sc(l, SC_LN1W), sc(l, SC_TMP), ALU.mult), reads=k, writes=wk)
                P.op("dve", I_tt(sc(l, SC_HB2 + 8 * w), sc(l, SC_LN1B), sc(l, SC_TMP), ALU.mult), reads=k, writes=wk)
                P.op("dve", I_tt(sc(l, SC_HB2 + 8 * w), sc(l, SC_HB2 + 8 * w), md[:, 24:32], ALU.add), reads=k, writes=wk)
            P.op("dve", I_ts(sc(l, SC_AW1), sc(l, SC_LN1W), ALPHA, None, ALU.mult), reads=k, writes=wk)
            P.op("dve", I_ts(sc(l, SC_AB1), sc(l, SC_LN1B), ALPHA, None, ALU.mult), reads=k, writes=wk)
            P.op("dve", I_ts(sc(l, SC_AW2), sc(l, SC_LN2W), ALPHA, None, ALU.mult), reads=k, writes=wk)
            P.op("dve", I_ts(sc(l, SC_AB2), sc(l, SC_LN2B), ALPHA, None, ALU.mult), reads=k, writes=wk)
        for l in range(n_layers - 1):
            k = ["sc%d" % l, "sc%d" % (l + 1)]
            wk = ["sc%d" % l]
            for w in range(2):
                P.op("dve", I_tt(sc(l, SC_HW1N + 8 * w), sc(l, SC_LN2W), sc(l + 1, SC_S1 + 8 * w), ALU.mult), reads=k, writes=wk)
                P.op("dve", I_tt(sc(l, SC_HB1N + 8 * w), sc(l, SC_LN2B), sc(l + 1, SC_S1 + 8 * w), ALU.mult), reads=k, writes=wk)
                P.op("dve", I_tt(sc(l, SC_HB1N + 8 * w), sc(l, SC_HB1N + 8 * w), MOD[:, l + 1, w, 0:8], ALU.add),
                     reads=k + ["mod%d" % (l + 1)], writes=wk)
        P.barrier()

        tap('sc', SC[:, 0, :], ['sc0'])
        phase('sc')
        TMP.reset()
        XT = [TMP.alloc([128, 1024], F32) for _ in range(2)]
        XS = [TMP.alloc([128, 8, 128], F32) for _ in range(2)]
        def in_load(j):
            src = dr['ctx'][j * 128:(j + 1) * 128, :] if j < 2 else dr['x'][(j - 2) * 128:(j - 1) * 128, :]
            q = j % 2
            P.dma("sp", I_dma(XT[q], src), writes=["xt%d" % q], sem_key="xt%d" % q)

        in_load(0)
        in_load(1)
        for j in range(18):
            w = 1 if j < 2 else 0
            q = j % 2
            bs = [nb(), nb()]
            for hb in range(2):
                P.op("pe", I_mms([(PS[bs[hb]][:, c * 128:(c + 1) * 128], XT[q][:, (hb * 4 + c) * 128:(hb * 4 + c + 1) * 128], IDENT, True, True)
                                  for c in range(4)]), reads=["xt%d" % q, "c_ident"], writes=[PSK[bs[hb]]])
            for kc in range(8):
                pv = PS[bs[kc // 4]][:, (kc % 4) * 128:(kc % 4 + 1) * 128]
                P.op("act", I_act(Ht[:, kc, j * 128:(j + 1) * 128], pv, AF.Identity,
                                  bias=MOD[:, 0, w, kc:kc + 1], scale=SC[:, 0, SC_S1 + 8 * w + kc:SC_S1 + 8 * w + kc + 1]),
                     reads=["sc0", "mod0"], writes=[PSK[bs[kc // 4]], "H%d" % j])
            for hb in range(2):
                P.op("dve", I_ts(XS[q][:, hb * 4:(hb + 1) * 4, :], PS[bs[hb]][:, :].rearrange("p (c t) -> p c t", c=4), ALPHA, None, ALU.mult),
                     writes=[PSK[bs[hb]], "xs%d" % q])
            P.dma("sp", I_dma(XA[:, :, j * 128:(j + 1) * 128].rearrange("kc p t -> p kc t"), XS[q]), reads=["xs%d" % q],
                  writes=["XA%d" % j], sem_key="xs%d" % q)
            if j + 2 < 18:
                in_load(j + 2)
        P.barrier()

        tap('h', Ht, ['H%d' % j for j in range(18)])
        phase('in')
        def hkeys(t0, n):
            return ["H%d" % j for j in range(t0 // 128, (t0 + n) // 128)]

        def xakeys(t0, n):
            return ["XA%d" % j for j in range(t0 // 128, (t0 + n) // 128)]

        def fm_group(bk, W_kc, src_kc, nk, t0, n, M=128):
            return I_mms([(PS[bk][0:M, 0:n], W_kc(kc), src_kc(kc)[:, t0:t0 + n], kc == 0, kc == nk - 1) for kc in range(nk)])

        def Hk(kc):
            return Ht[:, kc, :]

        for l in range(n_layers):
            last = (l == DEPTH - 1)
            tts = TTS[1:] if last else TTS
            tiles_out = list(range(2, 18)) if last else list(range(18))
            sck = "sc%d" % l
            AR.reset()
            TMP.reset()
            OB = AR.alloc([128, 4, NT], BF16, at=0)
            KR = AR.alloc([128, 2, NT], BF16, at=18432)
            VAE = AR.alloc([128, 18, 2, 128], BF16, at=27648)
            VAO = AR.alloc([128, 18, 2, 128], BF16, at=36864)
            G0 = 46080
            KGT = AR.alloc([128, 2, NT], BF16, at=G0)
            QGT = AR.alloc([128, 2, NT], BF16, at=G0 + 9216)
            GT = AR.alloc([33, NT], F32, at=G0 + 18432)
            KGTM = AR.alloc([128, 18, 256], BF16, at=G0 + 27648)
            VGTM = AR.alloc([128, 18, 512], BF16, at=G0 + 36864)
            KRAW = TMP.alloc([128, 512], F32)
            T1 = TMP.alloc([128, 512], F32)
            T2 = TMP.alloc([128, 512], F32)
            P.op("dve", I_memset(GT[32:33, :], 1.0), writes=["gt"])
            P.op("dve", I_memset(VAE, 0.0), writes=["vae"])
            P.op("dve", I_memset(VAO, 0.0), writes=["vao"])
            P.op("dve", I_memset(WUP, 0.0), writes=["wup"])
            P.dma("sp", I_dma(WUP[0:16, 0:256], dr['gla_gate_up_f'][l]), writes=["wup"], sem_key="wup")
            P.dma("sp", I_dma(WUP[16:32, 256:512], dr['gla_gate_up_b'][l]), writes=["wup"], sem_key="wup")
            P.dma("sp", I_dma(WUP[32:33, 0:256], dr['gla_gate_bias_f'][l:l + 1, :]), writes=["wup"], sem_key="wup")
            P.dma("sp", I_dma(WUP[32:33, 256:512], dr['gla_gate_bias_b'][l:l + 1, :]), writes=["wup"], sem_key="wup")
            P.dma("sp", I_dma(SINK[:, 0:8], dr['att_sink'][l:l + 1, :].partition_broadcast(128)), writes=["sink"], sem_key="sink")
            P.op("act", I_act(SINK[:, 8:16], SINK[:, 0:8], AF.Exp), reads=["sink"], writes=["sink"])
            for kv in range(2):
                P.op("dve", I_copy(SINKE[0:64, kv, :], SINK[0:64, 8 + kv * 4:8 + kv * 4 + 4:2]), reads=["sink"], writes=["sinke"])
                P.op("dve", I_copy(SINKE[64:128, kv, :], SINK[64:128, 8 + kv * 4 + 1:8 + kv * 4 + 4:2]), reads=["sink"], writes=["sinke"])

            vA, srcA = wsl(l, 'w_in', 0, 512)
            slotA, kA = ws.get([(vA, srcA)])
            WA = vA(slotA)
            for (t0, n) in TTS:
                for c in range(2):
                    b = nb()
                    P.op("pe", fm_group(b, lambda kc, c=c: WA[:, kc, c * 128:(c + 1) * 128], Hk, 8, t0, n),
                         reads=[kA] + hkeys(t0, n), writes=[PSK[b]])
                    P.op("act", I_act(KGT[:, c, t0:t0 + n], PS[b][:, 0:n], AF.Copy), writes=[PSK[b], "kgt"])
            for j in range(18):
                b = nb()
                P.op("pe", I_mms([(PS[b][:, 0:512], Ht[:, kc, j * 128:(j + 1) * 128], WA[:, kc, :], kc == 0, kc == 7) for kc in range(8)]),
                     reads=[kA, "H%d" % j], writes=[PSK[b]])
                P.op("act", I_act(KGTM[:, j, :], PS[b][:, 0:256], AF.Copy), writes=[PSK[b], "kgtm"])
                P.op("dve", I_copy(VGTM[:, j, 0:256], PS[b][:, 256:512]), writes=[PSK[b], "vgtm"])
            vB, srcB = wsl(l, 'w_in', 512, 512)
            slotB, kB = ws.get([(vB, srcB)])
            WB = vB(slotB)
            vC, srcC = wsl(l, 'w_in', 1024, 288)
            slotC, kCk = ws.get([(vC, srcC)], keep=1)
            WC = vC(slotC)
            for j in range(18):
                b = nb()
                lst = [(PS[b][:, 0:256], Ht[:, kc, j * 128:(j + 1) * 128], WB[:, kc, 0:256], kc == 0, kc == 7) for kc in range(8)]
                lst += [(PS[b][:, 256:352], Ht[:, kc, j * 128:(j + 1) * 128], WB[:, kc, 416:512], kc == 0, kc == 7) for kc in range(8)]
                lst += [(PS[b][:, 352:384], Ht[:, kc, j * 128:(j + 1) * 128], WC[:, kc, 0:32], kc == 0, kc == 7) for kc in range(8)]
                P.op("pe", I_mms(lst), reads=[kB, kCk, "H%d" % j], writes=[PSK[b]])
                P.op("dve", I_copy(VGTM[:, j, 256:512], PS[b][:, 0:256]), writes=[PSK[b], "vgtm"])
                pv = PS[b][:, 256:384].rearrange("p (k d) -> p k d", k=2)
                P.op("act", I_act(VAE[:, j, :, 0:64], pv, AF.Copy), writes=[PSK[b], "vae"])
                P.op("act", I_act(VAO[:, j, :, 64:128], pv, AF.Copy), writes=[PSK[b], "vao"])
            for (t0, n) in TTS:
                b = nb()
                P.op("pe", fm_group(b, lambda kc: WB[:, kc, 256:288], Hk, 8, t0, n, M=32), reads=[kB] + hkeys(t0, n), writes=[PSK[b]])
                P.op("act", I_act(GT[0:32, t0:t0 + n], PS[b][0:32, 0:n], AF.Copy), writes=[PSK[b], "gt"])
                for c in range(2):
                    b = nb()
                    P.op("pe", fm_group(b, lambda kc, c=c: WC[:, kc, 32 + c * 128:32 + (c + 1) * 128], Hk, 8, t0, n),
                         reads=[kCk] + hkeys(t0, n), writes=[PSK[b]])
                    P.op("act", I_act(QGT[:, c, t0:t0 + n], PS[b][:, 0:n], AF.Copy, scale=0.125), writes=[PSK[b], "qgt"])
                b = nb()
                P.op("pe", fm_group(b, lambda kc: WB[:, kc, 288:416], Hk, 8, t0, n), reads=[kB] + hkeys(t0, n), writes=[PSK[b]])
                P.op("act", I_act(KRAW[:, 0:n], PS[b][:, 0:n], AF.Copy), writes=[PSK[b], "kraw"])
                for kv in range(2):
                    bd, br_ = nb(), nb()
                    P.op("pe", I_mms([(PS[bd][:, 0:n], PERM[:, 1 + kv, :], KRAW[:, 0:n], True, True)]), reads=["kraw", "c_perm"], writes=[PSK[bd]])
                    P.op("pe", I_mms([(PS[br_][:, 0:n], PERM[:, 3 + kv, :], KRAW[:, 0:n], True, True)]), reads=["kraw", "c_perm"], writes=[PSK[br_]])
                    P.op("dve", I_tt(T1[:, 0:n], PS[bd][:, 0:n], ROPE[:, 0, t0:t0 + n], ALU.mult), reads=["c_rope"], writes=[PSK[bd], "t1"])
                    P.op("dve", I_tt(T2[:, 0:n], PS[br_][:, 0:n], ROPE[:, 1, t0:t0 + n], ALU.mult), reads=["c_rope"], writes=[PSK[br_], "t2"])
                    P.op("dve", I_tt(KR[:, kv, t0:t0 + n], T1[:, 0:n], T2[:, 0:n], ALU.add), reads=["t1", "t2"], writes=["kr"])
            P.barrier()

            if l == 0:
                tap('kgt', KGT, ['kgt']); tap('qgt', QGT, ['qgt']); tap('gt', GT, ['gt']); tap('kgtm', KGTM, ['kgtm'])
                tap('vgtm', VGTM, ['vgtm']); tap('kr', KR, ['kr']); tap('vae', VAE, ['vae']); tap('vao', VAO, ['vao'])
            phase('c1_%d' % l)
            TMP.reset()

            def gla_tmpset(alloc_fn):
                T_ = {}
                for nm, shp, dt_ in (("AZ", [128, 256], F32), ("GP", [128, 256], F32), ("EXE", [128, 256], F32), ("EB0", [128, 256], F32),
                                     ("ENB", [128, 256], F32), ("KDEC", [128, 2, 128], BF16), ("SBF", [128, 2, 2, 128], BF16),
                                     ("S", [128, 2, 128], F32), ("EB1", [128, 256], F32),
                                     ("KUPD4", [128, 2, 2, 2, 128], BF16),
                                     ("QBD0", [128, 2, 2, 128], BF16), ("QBD1", [128, 2, 2, 128], BF16),
                                     ("ST0", [128, 4, 128], BF16), ("ST1", [128, 4, 128], BF16),
                                     ("SQB", [128, 512], BF16), ("SQ", [128, 512], F32), ("OS", [128, 4, 128], F32)):
                    T_[nm] = alloc_fn(nm, shp, dt_)
                T_["RS"] = T_["SQ"]
                return T_

            Tf = gla_tmpset(lambda nm, shp, dt_: TMP.alloc(shp, dt_))
            if P.dry:
                sl1, sl2 = 1, 2
            else:
                sl1, sl2 = (ws.next - 2) % 3, (ws.next - 1) % 3
            cur = {"a": OFFS["wr%d" % sl1], "b": OFFS["wr%d" % sl2], "c": OFFS["row0"]}
            lim_ = {"a": OFFS["wr%d" % sl1] + 8192, "b": OFFS["wr%d" % sl2] + 8192, "c": OFFS["row0"] + 4096}
            place = dict(AZ="a", GP="a", EXE="a", EB0="a", ENB="a", KDEC="a", SBF="a", S="a", EB1="b", KUPD4="b",
                         QBD0="b", QBD1="b", ST0="b", ST1="b", SQB="b", SQ="c", OS="c")
            uniq[0] += 1

            def alloc_b(nm, shp, dt_):
                r_ = place[nm]
                nbytes = int(np.prod(shp[1:])) * (4 if dt_ == F32 else 2)
                t_ = nc.alloc_sbuf_tensor_at("glb_%s_%d" % (nm, uniq[0]), list(shp), dt_, offset=cur[r_])[:]
                cur[r_] += nbytes
                assert cur[r_] <= lim_[r_], (nm, r_)
                return t_

            Tb = gla_tmpset(alloc_b)
            alias_keys = ["wr%d" % sl1, "wr%d" % sl2]
            for d, T_ in ((0, Tf), (1, Tb)):
                P.op("dve", I_memset(T_["S"], 0.0), writes=["S%d_%d" % (d, cc_) for cc_ in range(2)])
                P.op("dve", I_memset(T_["KUPD4"], 0.0), writes=["kupd%d" % d])
                for p_ in range(2):
                    P.op("dve", I_memset(T_["QBD%d" % p_], 0.0), writes=["qdec%d%d" % (p_, d)])
            order_f = list(range(18))
            order_b = [1, 0] + list(range(17, 1, -1))
            step_of = {0: {j: i for i, j in enumerate(order_f)}, 1: {j: i for i, j in enumerate(order_b)}}

            def gla_tile(d, j, T_):
                par = step_of[d][j] % 2
                dbl = ("eb", "qdec", "st")
                k_ = lambda nm: ("%s%d%d" % (nm, par, d)) if nm in dbl else ("%s%d" % (nm, d))
                AZ, GP, EXE, ENB, KDEC = T_["AZ"], T_["GP"], T_["EXE"], T_["ENB"], T_["KDEC"]
                EB, KUPD4, QBD, ST = T_["EB%d" % par], T_["KUPD4"], T_["QBD%d" % par], T_["ST%d" % par]
                SBF, S, SQ, RS, OS, SQB = T_["SBF"], T_["S"], T_["SQ"], T_["RS"], T_["OS"], T_["SQB"]
                cs = slice(j * 128, (j + 1) * 128)
                do_out = j in tiles_out
                so, st_ = step_of[1 - d][j], step_of[d][j]
                second = (so < st_) or (so == st_ and d == 0)
                bz = nb()
                P.op("pe", I_mms([(PS[bz][:, 0:256], GT[0:33, cs], WUP[0:33, d * 256:(d + 1) * 256], True, True)]),
                     reads=["gt", "wup"], writes=[PSK[bz]])
                P.op("act", I_act(AZ, PS[bz][:, 0:256], AF.Exp, scale=-1.0), writes=[PSK[bz], k_("az")])
                P.op("act", I_act(GP, AZ, AF.Ln, bias=1.0), reads=[k_("az")], writes=[k_("gp")])
                yield 1
                be = nb()
                P.op("pe", I_mms([(PS[be][:, 0:256], TRI[:, 2 + d, :], GP, True, True)] +
                                 [(PS[be][:, 256 + c * 128:256 + (c + 1) * 128], GP[:, c * 128:(c + 1) * 128], TRI[:, d, :], True, True) for c in range(2)]),
                     reads=[k_("gp"), "c_tri"], writes=[PSK[be]])
                P.op("act", I_act(EXE, PS[be][:, 0:256], AF.Exp), writes=[PSK[be], k_("exe")])
                P.op("act", I_act(EB, PS[be][:, 256:512], AF.Exp), writes=[PSK[be], k_("eb")])
                if do_out:
                    P.op("act", I_act(ENB, PS[be][:, 256:512], AF.Exp, scale=-1.0), writes=[PSK[be], k_("enb")])
                yield 2
                for n_ in range(2):
                    for hf in range(2):
                        rows = slice(n_ * 64, (n_ + 1) * 64)
                        P.op("dve", I_tt(KUPD4[rows, n_, :, hf, hf * 64:(hf + 1) * 64],
                                         KGTM[rows, j, :].rearrange("p (cc hf k) -> p cc hf k", cc=2, hf=2)[:, :, hf, :],
                                         EXE[rows, :].rearrange("p (cc hf k) -> p cc hf k", cc=2, hf=2)[:, :, hf, :], ALU.mult),
                             reads=["kgtm", k_("exe")], writes=[k_("kupd")])
                if do_out:
                    for hf in range(2):
                        P.op("dve", I_tt(QBD[hf * 64:(hf + 1) * 64, :, hf, :], QGT[hf * 64:(hf + 1) * 64, :, cs],
                                         EB[hf * 64:(hf + 1) * 64, :].rearrange("p (c t) -> p c t", c=2), ALU.mult), reads=["qgt", k_("eb")], writes=[k_("qdec")])
                    P.op("dve", I_tt(KDEC, KGT[:, :, cs], ENB.rearrange("p (c t) -> p c t", c=2), ALU.mult), reads=["kgt", k_("enb")], writes=[k_("kdec")])
                    bs_ = nb()
                    P.op("pe", I_mms([(PS[bs_][:, cc * 256:(cc + 1) * 256], KDEC[:, cc, :], QBD[:, cc, :, :], True, True) for cc in range(2)]),
                         reads=[k_("kdec"), k_("qdec")], writes=[PSK[bs_]])
                    P.op("dve", I_tt(ST, PS[bs_][:, :].rearrange("p (h t) -> p h t", h=4),
                                     MSK[:, d, :].unsqueeze(1).broadcast_to([128, 4, 128]), ALU.mult), reads=["c_msk"], writes=[PSK[bs_], k_("st")])
                yield 3
                bu = [nb()]
                reserved.update(bu)
                lstu = []
                for cc in range(2):
                    for n_ in range(2):
                        blk = cc * 2 + n_
                        for hf in range(2):
                            lstu.append((PS[bu[0]][:, blk * 128:(blk + 1) * 128], KUPD4[:, n_, cc, hf, :],
                                         VGTM[:, j, (cc * 2 + hf) * 128:(cc * 2 + hf + 1) * 128], hf == 0, hf == 1))
                P.op("pe", I_mms(lstu), reads=[k_("kupd"), "vgtm"], writes=[PSK[bu[0]]])
                yield 4
                co = [0, 1] if d == 0 else [1, 0]
                for n_ in co:
                    yield 5
                    if do_out:
                        P.op("act", I_act(SBF[:, :, n_, :], S, AF.Copy), reads=["S%d_%d" % (d, cc_) for cc_ in range(2)],
                             writes=[k_("sbf%d" % n_)])
                    lcol = (63 if d == 0 else 0) + n_ * 64
                    for cc in range(2):
                        blk = cc * 2 + n_
                        P.op("dve", I_stt(S[:, cc, :], S[:, cc, :], EB[:, cc * 128 + lcol:cc * 128 + lcol + 1],
                                          PS[bu[0]][:, blk * 128:(blk + 1) * 128], ALU.mult, ALU.add),
                             reads=[k_("eb")], writes=[PSK[bu[0]], "S%d_%d" % (d, cc)])
                reserved.difference_update(bu)
                if not do_out:
                    return
                yield 6
                bo = nb()
                lst = []
                for h in range(4):
                    cc = h // 2
                    lst.append((PS[bo][:, h * 128:(h + 1) * 128], VGTM[:, j, h * 128:(h + 1) * 128], ST[:, h, :], True, False))
                    for n_ in range(2):
                        lst.append((PS[bo][:, h * 128 + n_ * 64:h * 128 + n_ * 64 + 64], SBF[:, cc, n_, :],
                                    QBD[:, cc, h % 2, n_ * 64:(n_ + 1) * 64], False, n_ == 1))
                P.op("pe", I_mms(lst), reads=["vgtm", k_("st"), k_("sbf0"), k_("sbf1"), k_("qdec")], writes=[PSK[bo]])
                pso = PS[bo][:, :].rearrange("p (h t) -> p h t", h=4)
                if not second:
                    P.op("act", I_act(OB[:, :, cs], pso, AF.Copy), writes=[PSK[bo], "ob%d" % j])
                else:
                    P.op("dve", I_tt(OS, pso, OB[:, :, cs], ALU.add), reads=["ob%d" % j], writes=[PSK[bo], k_("os")])
                    P.op("dve", I_tt(SQB, OS.rearrange("p h t -> p (h t)"), OS.rearrange("p h t -> p (h t)"), ALU.mult), reads=[k_("os")], writes=[k_("sq")])
                    bn_ = nb()
                    P.op("pe", I_mms([(PS[bn_][:, :], ONEEO[:, 0, :], SQB, True, False), (PS[bn_][:, :], ONEEO[:, 1, :], SQB, False, True)]),
                         reads=[k_("sq"), "c_oneeo"], writes=[PSK[bn_]])
                    P.op("act", I_act(RS, PS[bn_][:, :], AF.Ln, bias=RMS_EPS, scale=1.0 / 128.0), writes=[PSK[bn_], k_("rs")])
                    P.op("act", I_act(RS, RS, AF.Exp, scale=-0.5), writes=[k_("rs")])
                    P.op("dve", I_stt(OB[:, :, cs], OS, SC[:, l, SC_NORMW:SC_NORMW + 1], RS.rearrange("p (h t) -> p h t", h=4), ALU.mult, ALU.mult),
                         reads=[k_("os"), k_("rs"), sck], writes=["ob%d" % j])

            active = []
            nxt = [0]

            def start_pair():
                i_ = nxt[0]
                nxt[0] += 1
                active.append([gla_tile(1, order_b[i_], Tb), 0])
                active.append([gla_tile(0, order_f[i_], Tf), 0])

            start_pair()
            while active:
                for ent in list(active):
                    try:
                        ent[1] = next(ent[0])
                    except StopIteration:
                        active.remove(ent)
                if nxt[0] < 18 and (not active or active[-1][1] >= 2) and len(active) <= 2:
                    start_pair()
            P.barrier()
            P.op("dve", I_memset(Tf["S"], 0.0), writes=alias_keys + ["S0_0"])
            if l == 0:
                tap('yapre', OB, ['ob%d' % j for j in range(18)])
            phase('gla_%d' % l)
            TMP.reset()
            SIL = [TMP.alloc([128, 512], BF16) for _ in range(2)]
            vR, srcR = wsl(l, 'w_in', O_RG, 512)
            slotR, kR = ws.get([(vR, srcR)])
            WRg = vR(slotR)
            qq = 0
            for h in range(4):
                for (t0, n) in tts:
                    b = nb()
                    P.op("pe", fm_group(b, lambda kc, h=h: WRg[:, kc, h * 128:(h + 1) * 128], Hk, 8, t0, n), reads=[kR] + hkeys(t0, n), writes=[PSK[b]])
                    P.op("act", I_act(SIL[qq][:, 0:n], PS[b][:, 0:n], AF.Silu), writes=[PSK[b], "sil%d" % qq])
                    P.op("dve", I_tt(OB[:, h, t0:t0 + n], OB[:, h, t0:t0 + n], SIL[qq][:, 0:n], ALU.mult), reads=["sil%d" % qq], writes=["ya"])
                    qq ^= 1
            P.barrier()
            tap("ya%d" % l, OB, ["ya"])

            phase('rg_%d' % l)
            YB = AR.alloc([128, 4, NT], BF16, at=G0)
            YC = AR.alloc([128, 4, NT], BF16, at=G0 + 18432)
            X0 = G0 + 36864
            U = AR.alloc([128, NT], BF16, at=X0)
            V = AR.alloc([128, NT], F32, at=X0 + 4608)
            CB = AR.alloc([128, NT], BF16, at=X0 + 13824)
            TMP.reset()
            CHS = [TMP.alloc([128, 512], F32) for _ in range(2)]
            segs = [(C, NT)] if last else [(0, C), (C, NT)]
            a0 = C if last else 0
            qq = 0
            for jc in range(4):
                parts = []
                for i3, o in enumerate((O_CH, O_CB, O_CC)):
                    src = dr['w_in'][l, :, o + jc * 128:o + (jc + 1) * 128].rearrange("(kc p) c -> p kc c", p=128)
                    parts.append(((lambda s, i3=i3: s[:, 0:3072].rearrange("p (kc b c) -> p kc b c", kc=8, b=3)[:, :, i3, :]), src))
                slot, sk = ws.get(parts)
                W3 = slot[:, 0:3072].rearrange("p (kc b c) -> p kc b c", kc=8, b=3)
                for (t0, n) in tts:
                    bh, bb_, bc = nb(), nb(), nb()
                    for i3, bx in enumerate((bh, bb_, bc)):
                        P.op("pe", fm_group(bx, lambda kc, i3=i3: W3[:, kc, i3, :], Hk, 8, t0, n), reads=[sk] + hkeys(t0, n), writes=[PSK[bx]])
                    P.op("act", I_act(CHS[qq][:, 0:n], PS[bh][:, 0:n], AF.Copy), writes=[PSK[bh], "chs%d" % qq])
                    P.op("dve", I_tt(U[:, t0:t0 + n], PS[bc][:, 0:n], CHS[qq][:, 0:n], ALU.mult), reads=["chs%d" % qq], writes=[PSK[bc], "U"])
                    P.op("act", I_act(CB[:, t0:t0 + n], PS[bb_][:, 0:n], AF.Copy), writes=[PSK[bb_], "CB"])
                    qq ^= 1
                w0 = SC[:, l, SC_CONV + 0 * 4 + jc:SC_CONV + 0 * 4 + jc + 1]
                w1 = SC[:, l, SC_CONV + 1 * 4 + jc:SC_CONV + 1 * 4 + jc + 1]
                w2 = SC[:, l, SC_CONV + 2 * 4 + jc:SC_CONV + 2 * 4 + jc + 1]
                P.op("dve", I_ts(V[:, a0:NT], U[:, a0:NT], w1, None, ALU.mult), reads=["U", sck], writes=["V"])
                for (s0, s1) in segs:
                    P.op("dve", I_stt(V[:, s0 + 1:s1], U[:, s0:s1 - 1], w0, V[:, s0 + 1:s1], ALU.mult, ALU.add), reads=["U", sck], writes=["V"])
                    P.op("dve", I_stt(V[:, s0:s1 - 1], U[:, s0 + 1:s1], w2, V[:, s0:s1 - 1], ALU.mult, ALU.add), reads=["U", sck], writes=["V"])
                P.op("dve", I_tt(YB[:, jc, a0:NT], CB[:, a0:NT], V[:, a0:NT], ALU.mult), reads=["CB", "V"], writes=["yb"])
            P.barrier()
            tap("yb%d" % l, YB, ["yb"])

            phase('conv_%d' % l)
            QRe = AR.alloc([128, 2, NT], BF16, at=X0)
            QRo = AR.alloc([128, 2, NT], BF16, at=X0 + 9216)
            TMP.reset()
            NPT = 10
            PT = [TMP.alloc([128, 512], BF16) for i in range(NPT)]
            P.op("dve", I_memset(QRe[64:128, :, :], 0.0), writes=["qr"])
            P.op("dve", I_memset(QRo[0:64, :, :], 0.0), writes=["qr"])
            QRAW = TMP.alloc([128, 512], F32)
            T1 = TMP.alloc([128, 512], F32)
            T2 = TMP.alloc([128, 512], F32)
            DN = TMP.alloc([128, 2, 128], F32)
            RD = TMP.alloc([128, 2, 128], F32)
            pti = 0
            for kv in range(2):
                vQ, srcQ = wsl(l, 'w_in', O_QA + kv * 256, 256)
                slotQ, kQ = ws.get([(vQ, srcQ)])
                WQ = vQ(slotQ)
                for (t0, n) in tts:
                    for c in range(2):
                        b = nb()
                        P.op("pe", fm_group(b, lambda kc, c=c: WQ[:, kc, c * 128:(c + 1) * 128], Hk, 8, t0, n), reads=[kQ] + hkeys(t0, n), writes=[PSK[b]])
                        P.op("act", I_act(QRAW[:, 0:n], PS[b][:, 0:n], AF.Copy, scale=0.125), writes=[PSK[b], "qraw"])
                        b2 = nb()
                        P.op("pe", I_mms([(PS[b2][:, 0:n], PERM[:, 0, :], QRAW[:, 0:n], True, True)]), reads=["qraw", "c_perm"], writes=[PSK[b2]])
                        P.op("dve", I_tt(T1[:, 0:n], QRAW[:, 0:n], ROPE[:, 0, t0:t0 + n], ALU.mult), reads=["qraw", "c_rope"], writes=["t1"])
                        P.op("dve", I_tt(T2[:, 0:n], PS[b2][:, 0:n], ROPE[:, 1, t0:t0 + n], ALU.mult), reads=["c_rope"], writes=[PSK[b2], "t2"])
                        P.op("dve", I_tt(QRe[0:64, c, t0:t0 + n], T1[0:64, 0:n], T2[0:64, 0:n], ALU.add), reads=["t1", "t2"], writes=["qr"])
                        P.op("dve", I_tt(QRo[64:128, c, t0:t0 + n], T1[64:128, 0:n], T2[64:128, 0:n], ALU.add), reads=["t1", "t2"], writes=["qr"])
                blocks = ([] if last else [0, 1]) + list(range(2, 18))
                def st_phase(jb):
                    nonlocal pti
                    qs = slice(jb * 128, (jb + 1) * 128)
                    if jb < 2:
                        kts = [(0, None), (1, None)]
                    else:
                        kts = []
                        if jb > 2:
                            kts.append((jb - 1, 2))
                        kts.append((jb, None))
                        if jb < 17:
                            kts.append((jb + 1, 3))
                        kts += [(0, None), (1, None)]
                    used = []
                    for (kt, mk) in kts:
                        ks = slice(kt * 128, (kt + 1) * 128)
                        b = nb()
                        P.op("pe", I_mms([(PS[b][:, 0:256], KR[:, kv, ks], QRe[:, :, qs], True, True),
                                          (PS[b][:, 256:512], KR[:, kv, ks], QRo[:, :, qs], True, True)]),
                             reads=["kr", "qr"], writes=[PSK[b]])
                        pk = "pt%d" % pti
                        P.op("act", I_act(PT[pti], PS[b][:, :], AF.Exp), writes=[PSK[b], pk])
                        if mk is not None:
                            P.op("dve", I_tt(PT[pti].rearrange("p (g t) -> p g t", g=4), PT[pti].rearrange("p (g t) -> p g t", g=4),
                                             MSK[:, mk, :].unsqueeze(1).broadcast_to([128, 4, 128]), ALU.mult), reads=["c_msk"], writes=[pk])
                        used.append((kt, pti))
                        pti = (pti + 1) % NPT
                    return used

                def pv_phase(jb, used):
                    qs = slice(jb * 128, (jb + 1) * 128)
                    bo, bd = nb(), nb()
                    lo, ld = [], []
                    nk = len(used)
                    for ii, (kt, pi) in enumerate(used):
                        lo.append((PS[bo][:, 0:256], VAE[:, kt, kv, :], PT[pi][:, 0:256], ii == 0, False))
                        lo.append((PS[bo][:, 0:256], VAO[:, kt, kv, :], PT[pi][:, 256:512], False, ii == nk - 1))
                        ld.append((PS[bd][:, 0:256], ONEEO[:, 0, :], PT[pi][:, 0:256], ii == 0, False))
                        ld.append((PS[bd][:, 0:256], ONEEO[:, 1, :], PT[pi][:, 256:512], False, ii == nk - 1))
                    pks = ["pt%d" % pi for (_, pi) in used]
                    P.op("pe", I_mms(lo), reads=["vae", "vao"] + pks, writes=[PSK[bo]])
                    P.op("pe", I_mms(ld), reads=["c_oneeo"] + pks, writes=[PSK[bd]])
                    P.op("dve", I_tt(DN, PS[bd][:, 0:256].rearrange("p (c t) -> p c t", c=2),
                                     SINKE[:, kv, :].unsqueeze(2).broadcast_to([128, 2, 128]), ALU.add), reads=["sinke"], writes=[PSK[bd], "dn"])
                    P.op("dve", I_recip(RD, DN), reads=["dn"], writes=["rd"])
                    P.op("dve", I_tt(YC[:, kv * 2:kv * 2 + 2, qs], PS[bo][:, 0:256].rearrange("p (c t) -> p c t", c=2), RD, ALU.mult),
                         reads=["rd"], writes=[PSK[bo], "yc"])

                prev_ = None
                for jb in blocks:
                    used_ = st_phase(jb)
                    if prev_ is not None:
                        pv_phase(*prev_)
                    prev_ = (jb, used_)
                pv_phase(*prev_)
            P.barrier()
            tap("yc%d" % l, YC, ["yc"])

            phase('attn_%d' % l)
            Mch = [AR.alloc([128, NT], BF16, at=18432 + i * 4608) for i in range(6)] + \
                  [AR.alloc([128, NT], BF16, at=X0 + i * 4608) for i in range(2)]
            TMP.reset()
            SG = [TMP.alloc([128, 512], F32) for _ in range(3)]
            TT_ = [TMP.alloc([128, 512], F32) for _ in range(3)]
            Ybr = [OB, YB, YC]
            for i in range(8):
                pg, pw = [], []
                for br in range(3):
                    srcg = dr['w_in'][l, :, O_MA + br * 1024 + i * 128:O_MA + br * 1024 + (i + 1) * 128].rearrange("(kc p) c -> p kc c", p=128)
                    pg.append(((lambda s, br=br: s[:, 0:3072].rearrange("p (kc b c) -> p kc b c", kc=8, b=3)[:, :, br, :]), srcg))
                    wn = ('w_branch_a', 'w_branch_b', 'w_branch_c')[br]
                    srcw = dr[wn][l, :, i * 128:(i + 1) * 128].rearrange("(kc p) c -> p kc c", p=128)
                    pw.append(((lambda s, br=br: s[:, 0:1536].rearrange("p (kc b c) -> p kc b c", kc=4, b=3)[:, :, br, :]), srcw))
                slotg, kg_ = ws.get(pg)
                slotw, kw_ = ws.get(pw, keep=1)
                WG = slotg[:, 0:3072].rearrange("p (kc b c) -> p kc b c", kc=8, b=3)
                WBr = slotw[:, 0:1536].rearrange("p (kc b c) -> p kc b c", kc=4, b=3)
                for (t0, n) in tts:
                    bg = [nb(), nb(), nb()]
                    bp = [nb(), nb(), nb()]
                    for br in range(3):
                        P.op("pe", fm_group(bg[br], lambda kc, br=br: WG[:, kc, br, :], Hk, 8, t0, n), reads=[kg_] + hkeys(t0, n), writes=[PSK[bg[br]]])
                        P.op("pe", fm_group(bp[br], lambda kc, br=br: WBr[:, kc, br, :], lambda kc, br=br: Ybr[br][:, kc, :], 4, t0, n),
                             reads=[kw_, ("ya", "yb", "yc")[br]], writes=[PSK[bp[br]]])
                        P.op("act", I_act(SG[br][:, 0:n], PS[bg[br]][:, 0:n], AF.Sigmoid), writes=[PSK[bg[br]], "sg%d" % br])
                        P.op("dve", I_tt(TT_[br][:, 0:n], PS[bp[br]][:, 0:n], SG[br][:, 0:n], ALU.mult), reads=["sg%d" % br], writes=[PSK[bp[br]], "tt%d" % br])
                    P.op("dve", I_tt(TT_[0][:, 0:n], TT_[0][:, 0:n], TT_[1][:, 0:n], ALU.add), reads=["tt1"], writes=["tt0"])
                    P.op("dve", I_tt(Mch[i][:, t0:t0 + n], TT_[0][:, 0:n], TT_[2][:, 0:n], ALU.add), reads=["tt0", "tt2"], writes=["m%d" % i])
            P.barrier()
            tap("m%d" % l, Mch[0], ["m0"])

            phase('merge_%d' % l)
            def add_pass(wparts_fn, nk, src_kc, src_keys, gcol):
                TMP.reset()
                NB_ = 4
                XJ = [TMP.alloc([128, 512], F32) for _ in range(NB_)]
                RO = [TMP.alloc([128, 512], F32) for _ in range(NB_)]
                items = [(j, t0, n) for j in range(8) for (t0, n) in tts]

                def load(it):
                    j, t0, n = items[it]
                    q_ = it % NB_
                    P.dma("sp", I_dma(XJ[q_][:, 0:n], XA[j, :, t0:t0 + n]), reads=["XA_%d_%d" % (j, t0)], writes=["xj%d" % q_], sem_key="xj%d" % q_)

                for it in range(min(NB_ - 1, len(items))):
                    load(it)
                Wj, wkeys, jcur = None, None, -1
                for it, (j, t0, n) in enumerate(items):
                    if j != jcur:
                        Wj, wkeys = wparts_fn(j)
                        jcur = j
                    who = 1 if t0 < C else 0
                    q_ = it % NB_
                    b = nb()
                    P.op("pe", fm_group(b, Wj, src_kc, nk, t0, n), reads=wkeys + src_keys, writes=[PSK[b]])
                    P.op("dve", I_stt(RO[q_][:, 0:n], PS[b][:, 0:n], MOD[:, l, who, gcol * 8 + j:gcol * 8 + j + 1], XJ[q_][:, 0:n], ALU.mult, ALU.add),
                         reads=["xj%d" % q_, "mod%d" % l], writes=[PSK[b], "ro%d" % q_])
                    P.dma("act", I_dma(XA[j, :, t0:t0 + n], RO[q_][:, 0:n]), reads=["ro%d" % q_], writes=["XA_%d_%d" % (j, t0)], sem_key="ro%d" % q_)
                    if it + NB_ - 1 < len(items):
                        load(it + NB_ - 1)
                P.barrier()

            def ln_pass(aw, ab, hw, hb, final=False):
                AR.reset()
                TMP.reset()
                RSg = [AR.alloc([128, 8, 512], F32) for _ in range(2)]
                XO = [AR.alloc([128, 8, 512], F32) for _ in range(2)]
                SQl = [AR.alloc([128, 512], F32) for _ in range(2)]
                XH = [AR.alloc([128, 512], F32) for _ in range(4)]
                OT = [AR.alloc([128, 1024], F32) for _ in range(2)]
                MEANs = [TMP.alloc([128, 512], F32) for _ in range(2)]
                MSQs = [TMP.alloc([128, 512], F32) for _ in range(2)]
                VARs = [TMP.alloc([128, 512], F32) for _ in range(2)]
                RSTDs = [TMP.alloc([128, 512], F32) for _ in range(2)]
                tl = [(t0, n) for (t0, n) in tts if not (final and t0 < C)]
                st = dict(sq=0, xh=0, ot=0)
                banks = {}

                def load(i):
                    t0, n = tl[i]
                    q_ = i % 2
                    P.dma("sp", I_dma(RSg[q_][:, :, 0:n], XA[:, :, t0:t0 + n].rearrange("kc p t -> p kc t")),
                          reads=["XA_%d_%d" % (j, t0) for j in range(8)], writes=["rsg%d" % q_], sem_key="rsg%d" % q_)

                def stats_pe(i):
                    t0, n = tl[i]
                    q_ = i % 2
                    rk = "rsg%d" % q_
                    b1, b2 = nb(), nb()
                    banks[i] = (b1, b2)
                    reserved.add(b1)
                    reserved.add(b2)
                    P.op("pe", I_mms([(PS[b1][:, 0:n], ONES, RSg[q_][:, kc, 0:n], kc == 0, kc == 7) for kc in range(8)]), reads=[rk, "c_ones"], writes=[PSK[b1]])
                    for kc in range(8):
                        si = st['sq']
                        P.op("act", I_act(SQl[si][:, 0:n], RSg[q_][:, kc, 0:n], AF.Square), reads=[rk], writes=["sql%d" % si])
                        P.op("pe", I_mms([(PS[b2][:, 0:n], ONES, SQl[si][:, 0:n], kc == 0, kc == 7)]), reads=["sql%d" % si, "c_ones"], writes=[PSK[b2]])
                        st['sq'] = si ^ 1

                def stats_dve(i):
                    t0, n = tl[i]
                    q_ = i % 2
                    b1, b2 = banks[i]
                    MEAN, MSQ, VAR, RSTD = MEANs[q_], MSQs[q_], VARs[q_], RSTDs[q_]
                    kmean, kmsq, kvar, krstd = "mean%d" % q_, "msq%d" % q_, "var%d" % q_, "rstd%d" % q_
                    P.op("dve", I_ts(MEAN[:, 0:n], PS[b1][:, 0:n], 1.0 / D, None, ALU.mult), writes=[PSK[b1], kmean])
                    P.op("dve", I_tt(MSQ[:, 0:n], MEAN[:, 0:n], MEAN[:, 0:n], ALU.mult), reads=[kmean], writes=[kmsq])
                    P.op("dve", I_stt(VAR[:, 0:n], PS[b2][:, 0:n], 1.0 / D, MSQ[:, 0:n], ALU.mult, ALU.subtract), reads=[kmsq], writes=[PSK[b2], kvar])
                    P.op("act", I_act(VAR[:, 0:n], VAR[:, 0:n], AF.Sqrt, bias=LN_EPS), writes=[kvar])
                    P.op("dve", I_recip(RSTD[:, 0:n], VAR[:, 0:n]), reads=[kvar], writes=[krstd])
                    reserved.discard(b1)
                    reserved.discard(b2)

                def norm(i):
                    t0, n = tl[i]
                    who = 1 if t0 < C else 0
                    q_ = i % 2
                    rk, xk = "rsg%d" % q_, "xo%d" % q_
                    MEAN, RSTD = MEANs[q_], RSTDs[q_]
                    kmean, krstd = "mean%d" % q_, "rstd%d" % q_
                    for kc in range(8):
                        xi = st['xh']
                        xhk = "xh%d" % xi
                        eng_ = "dve"
                        P.op(eng_, I_tt(XH[xi][:, 0:n], RSg[q_][:, kc, 0:n], MEAN[:, 0:n], ALU.subtract), reads=[rk, kmean], writes=[xhk])
                        P.op(eng_, I_tt(XH[xi][:, 0:n], XH[xi][:, 0:n], RSTD[:, 0:n], ALU.mult), reads=[krstd], writes=[xhk])
                        P.op(eng_, I_ts(XO[q_][:, kc, 0:n], XH[xi][:, 0:n], SC[:, l, aw + kc:aw + kc + 1], SC[:, l, ab + kc:ab + kc + 1], ALU.mult, ALU.add),
                             reads=[xhk, sck], writes=[xk])
                        if not final:
                            P.op("act", I_act(Ht[:, kc, t0:t0 + n], XH[xi][:, 0:n], AF.Identity,
                                              bias=SC[:, l, hb + 8 * who + kc:hb + 8 * who + kc + 1], scale=SC[:, l, hw + 8 * who + kc:hw + 8 * who + kc + 1]),
                                 reads=[xhk, sck], writes=hkeys(t0, n))
                        st['xh'] = (xi + 1) % 4
                    if not final:
                        P.dma("pool", I_dma(XA[:, :, t0:t0 + n].rearrange("kc p t -> p kc t"), XO[q_][:, :, 0:n]), reads=[xk],
                              writes=["XA_%d_%d" % (j, t0) for j in range(8)], sem_key=xk)
                    else:
                        for s_ in range(n // 128):
                            bs2 = [nb(), nb()]
                            for hb_ in range(2):
                                P.op("pe", I_mms([(PS[bs2[hb_]][:, c * 128:(c + 1) * 128], XO[q_][:, hb_ * 4 + c, s_ * 128:(s_ + 1) * 128], IDENT, True, True)
                                                  for c in range(4)]), reads=[xk, "c_ident"], writes=[PSK[bs2[hb_]]])
                            oi = st['ot']
                            ok = "ot%d" % oi
                            P.op("act", I_act(OT[oi][:, 0:512], PS[bs2[0]][:, :], AF.Copy), writes=[PSK[bs2[0]], ok])
                            P.op("dve", I_copy(OT[oi][:, 512:1024], PS[bs2[1]][:, :]), writes=[PSK[bs2[1]], ok])
                            r0 = t0 - C + s_ * 128
                            P.dma("sp", I_dma(out[r0:r0 + 128, :], OT[oi]), reads=[ok], sem_key=ok)
                            st['ot'] = oi ^ 1

                nt_ = len(tl)
                load(0)
                if nt_ > 1:
                    load(1)
                stats_pe(0)
                stats_dve(0)
                for i in range(nt_):
                    if i + 1 < nt_:
                        stats_pe(i + 1)
                    norm(i)
                    if i + 1 < nt_:
                        stats_dve(i + 1)
                    if i + 2 < nt_:
                        load(i + 2)
                P.barrier(engs=("pe", "act", "dve", "sp", "pool"))

            def wout_parts(j, cache={}):
                hf = j // 4
                if hf not in cache:
                    vO, srcO = wsl(l, 'w_out', hf * 512, 512)
                    slotO, kO = ws.get([(vO, srcO)])
                    cache.clear()
                    cache[hf] = (vO(slotO), kO)
                WO, kO = cache[hf]
                return (lambda kc, j=j: WO[:, kc, (j % 4) * 128:(j % 4 + 1) * 128]), [kO]

            add_pass(wout_parts, 8, lambda kc: Mch[kc], ["m%d" % i for i in range(8)], 2)
            ln_pass(SC_AW1, SC_AB1, SC_HW2, SC_HB2)

            phase('ln1_%d' % l)
            AR.reset()
            TMP.reset()
            Ach = [AR.alloc([128, NT], BF16) for _ in range(NFF)]
            UG = TMP.alloc([128, NT], BF16)
            UV = TMP.alloc([128, NT], BF16)
            VG = TMP.alloc([128, NT], BF16)
            VV = TMP.alloc([128, NT], BF16)
            PTMP = TMP.alloc([128, 512], BF16)
            for g in range(NFF // 2):
                parts = []
                for i2, o in enumerate((0, DFF)):
                    src = dr['ffn_up'][l, :, o + g * 256:o + (g + 1) * 256].rearrange("(kc p) c -> p kc c", p=128)
                    parts.append(((lambda s, i2=i2: s[:, 0:4096].rearrange("p (kc b c) -> p kc b c", kc=8, b=2)[:, :, i2, :]), src))
                slot, sk = ws.get(parts)
                WU = slot[:, 0:4096].rearrange("p (kc b c) -> p kc b c", kc=8, b=2)
                for c in range(2):
                    jf = g * 2 + c
                    cg, cv_ = jf, NFF + jf
                    wg = [SC[:, l, SC_FCONV + k * 44 + cg:SC_FCONV + k * 44 + cg + 1] for k in range(3)]
                    wv = [SC[:, l, SC_FCONV + k * 44 + cv_:SC_FCONV + k * 44 + cv_ + 1] for k in range(3)]

                    def seg_of(t0):
                        return (0, C) if t0 < C else (C, NT)

                    def conv_tile(ti):
                        t0, n = tts[ti]
                        s0, s1 = seg_of(t0)
                        lo_ = t0 if t0 > s0 else s0 + 1
                        hi_ = t0 + n if t0 + n < s1 else s1 - 1
                        nbr = ["%d" % x for x in (ti - 1, ti, ti + 1) if 0 <= x < len(tts)]
                        for (Ub, Vb, un, vn, wt, on_pool) in ((UG, VG, "ug", "vg", wg, False), (UV, VV, "uv", "vv", wv, True)):
                            P.op("dve", I_stt(Vb[:, lo_:t0 + n], Ub[:, lo_ - 1:t0 + n - 1], wt[0], Vb[:, lo_:t0 + n], ALU.mult, ALU.add),
                                 reads=[un + x for x in nbr] + [sck], writes=[vn + "%d" % ti])
                            if False:
                                m_ = hi_ - t0
                                P.op("pool", I_ts(PTMP[:, 0:m_], Ub[:, t0 + 1:hi_ + 1], wt[2], None, ALU.mult), reads=[un + x for x in nbr] + [sck], writes=["ptmp"])
                                P.op("pool", I_tt(Vb[:, t0:hi_], Vb[:, t0:hi_], PTMP[:, 0:m_], ALU.add), reads=["ptmp"], writes=[vn + "%d" % ti])
                            else:
                                P.op("dve", I_stt(Vb[:, t0:hi_], Ub[:, t0 + 1:hi_ + 1], wt[2], Vb[:, t0:hi_], ALU.mult, ALU.add),
                                     reads=[un + x for x in nbr] + [sck], writes=[vn + "%d" % ti])
                        P.op("act", I_act(Ach[jf][:, t0:t0 + n], VG[:, t0:t0 + n], AF.Silu), reads=["vg%d" % ti], writes=["a%d" % jf])
                        P.op("dve", I_tt(Ach[jf][:, t0:t0 + n], Ach[jf][:, t0:t0 + n], VV[:, t0:t0 + n], ALU.mult), reads=["vv%d" % ti], writes=["a%d" % jf])

                    pending = []
                    for ti, (t0, n) in enumerate(tts):
                        b1, b2 = nb(), nb()
                        P.op("pe", fm_group(b1, lambda kc, c=c: WU[:, kc, 0, c * 128:(c + 1) * 128], Hk, 8, t0, n), reads=[sk] + hkeys(t0, n), writes=[PSK[b1]])
                        P.op("pe", fm_group(b2, lambda kc, c=c: WU[:, kc, 1, c * 128:(c + 1) * 128], Hk, 8, t0, n), reads=[sk] + hkeys(t0, n), writes=[PSK[b2]])
                        P.op("act", I_act(VG[:, t0:t0 + n], PS[b1][:, 0:n], AF.Identity, scale=wg[1]), reads=[sck], writes=[PSK[b1], "vg%d" % ti])
                        P.op("act", I_act(UG[:, t0:t0 + n], PS[b1][:, 0:n], AF.Copy), writes=[PSK[b1], "ug%d" % ti])
                        P.op("act", I_act(VV[:, t0:t0 + n], PS[b2][:, 0:n], AF.Identity, scale=wv[1]), reads=[sck], writes=[PSK[b2], "vv%d" % ti])
                        P.op("act", I_act(UV[:, t0:t0 + n], PS[b2][:, 0:n], AF.Copy), writes=[PSK[b2], "uv%d" % ti])
                        pending.append(ti)
                        ready = []
                        for pt_ in pending:
                            p0, pn = tts[pt_]
                            if p0 + pn >= seg_of(p0)[1] or pt_ < ti:
                                ready.append(pt_)
                        for pt_ in ready:
                            pending.remove(pt_)
                            conv_tile(pt_)
                    for pt_ in pending:
                        conv_tile(pt_)
            P.barrier(engs=("pe", "act", "dve", "sp", "pool"))
            tap("a%d" % l, Ach[0], ["a0"])

            phase('ffnup_%d' % l)
            def wdown_parts(j):
                src = dr['ffn_down'][l, :, j * 128:(j + 1) * 128].rearrange("(kc p) c -> p kc c", p=128)
                vf = (lambda s: s[:, 0:NFF * 128].rearrange("p (kc c) -> p kc c", kc=NFF))
                slot, sk = ws.get([(vf, src)])
                Wd = vf(slot)
                return (lambda kc: Wd[:, kc, :]), [sk]

            add_pass(wdown_parts, NFF, lambda kc: Ach[kc], ["a%d" % i for i in range(NFF)], 5)
            if last:
                ln_pass(SC_LN2W, SC_LN2B, 0, 0, final=True)
            elif l == n_layers - 1:
                ln_pass(SC_LN2W, SC_LN2B, 0, 0, final=True)
            else:
                ln_pass(SC_AW2, SC_AB2, SC_HW1N, SC_HB1N)

        P.final_wait()
        return ws

    Pd = Prog(nc, dry=True)
    wsd = build(Pd)
    global ALLW
    ALLW = wsd.descs
    P = Prog(nc, dry=False)
    build(P)
    print("ops:", {e: len(P.ops[e]) for e in ENGS}, "weight loads:", len(ALLW))
    stack = ExitStack()
    P.emit(stack)
    stack.close()
    return nc


ALLW = []
_CACHE = {}


def kernel(**inputs):
    x = np.ascontiguousarray(inputs['x'], dtype=np.float32)
    c = np.asarray(inputs['c'], dtype=np.float32)
    ctx = np.ascontiguousarray(inputs['ctx'], dtype=np.float32)
    c_ctx = np.asarray(inputs['c_ctx'], dtype=np.float32)
    consts = make_consts()
    if 'nc' not in _CACHE:
        _CACHE['nc'] = build_program()
    nc = _CACHE['nc']
    shared = {k: np.ascontiguousarray(inputs[k], dtype=np.float32) for k in WEIGHT_SHAPES}
    shared.update(consts)
    in_maps = []
    for b in range(8):
        m = dict(shared)
        m['x'] = x[b]
        m['ctx'] = ctx[b]
        m['cvec'] = np.ascontiguousarray(np.stack([c[b], c_ctx], 0))
        in_maps.append(m)
    res = run_bass_kernel_spmd(nc, in_maps, core_ids=list(range(8)))
    return np.stack([np.asarray(r['out'], dtype=np.float32) for r in res.results], 0)
```
